# Optimizing a Trainium2 kernel written in Bass

```python
import jax, jax.numpy as jnp
from jax import lax
import numpy as np

D_MODEL = 2048
BATCH = 8
SEQ = 2048
DEPTH = 1

CHUNK = 64
HEAD_DIM = 128
N_HEADS_SB = 8
N_HEADS_CA = 8
W_SB = N_HEADS_SB * HEAD_DIM
W_CA = N_HEADS_CA * HEAD_DIM
LEFT_CHUNKS = 8
BAND = (LEFT_CHUNKS + 1) * CHUNK
REL_CLIP = 128
N_REL = REL_CLIP + CHUNK
Q_BLOCK = 128
D_FF = -(-8 * D_MODEL // (3 * 256)) * 256
D_PLE = 256
EPS = 1e-6
NEG = -1e30
IN_COLS = 3 * W_SB + 3 * W_CA + 2 * D_MODEL

kernel_name = "hybrid_stickbreak_chunkrel_block"


def rmsnorm(x, g):
    xf = x.astype(jnp.float32)
    y = xf * lax.rsqrt(jnp.mean(xf * xf, axis=-1, keepdims=True) + EPS)
    return (y * g.astype(jnp.float32)).astype(x.dtype)


def stick_breaking_attention(q, k, v):
    B, S, H, Dh = q.shape
    scale = Dh ** -0.5
    outs = []
    for qb in range(S // Q_BLOCK):
        t0 = qb * Q_BLOCK
        t1 = t0 + Q_BLOCK
        kb = k[:, :t1]
        vb = v[:, :t1]
        z = jnp.einsum('bqhd,bkhd->bhqk', q[:, t0:t1], kb).astype(jnp.float32) * scale
        past = jnp.arange(t1)[None, :] < jnp.arange(t0, t1)[:, None]
        log_keep = jnp.where(past, jax.nn.log_sigmoid(-z), 0.0)
        between = lax.cumsum(log_keep, axis=3, reverse=True) - log_keep
        a = jnp.where(past, jnp.exp(jax.nn.log_sigmoid(z) + between), 0.0)
        outs.append(jnp.einsum('bhqk,bkhd->bqhd', a.astype(v.dtype), vb))
    return jnp.concatenate(outs, axis=1)


def chunked_relpos_attention(q, k, v, rel_bias):
    B, S, H, Dh = q.shape
    nc = S // CHUNK
    pad = LEFT_CHUNKS * CHUNK
    scale = Dh ** -0.5
    kp = jnp.pad(k, ((0, 0), (pad, 0), (0, 0), (0, 0)))
    vp = jnp.pad(v, ((0, 0), (pad, 0), (0, 0), (0, 0)))
    qc = jnp.moveaxis(q.reshape(B, nc, CHUNK, H, Dh), 1, 0)
    s_loc = jnp.arange(BAND)[None, :]
    rel = s_loc - (jnp.arange(CHUNK)[:, None] + pad)
    rel_idx = jnp.clip(rel, -REL_CLIP, CHUNK - 1) + REL_CLIP
    bias = rel_bias.astype(jnp.float32)[:, rel_idx]

    def one_chunk(args):
        c, qblk = args
        start = c * CHUNK
        kb = lax.dynamic_slice_in_dim(kp, start, BAND, axis=1)
        vb = lax.dynamic_slice_in_dim(vp, start, BAND, axis=1)
        valid = (start + s_loc) >= pad
        z = jnp.einsum('bqhd,bkhd->bhqk', qblk, kb).astype(jnp.float32) * scale + bias
        w = jax.nn.softmax(jnp.where(valid, z, NEG), axis=-1)
        return jnp.einsum('bhqk,bkhd->bqhd', w.astype(v.dtype), vb)

    out = lax.map(one_chunk, (jnp.arange(nc), qc))
    return jnp.moveaxis(out, 0, 1).reshape(B, S, H, Dh)


def setup_inputs(seed: int = 0) -> dict:
    key = jax.random.key(seed)
    ks = jax.random.split(key, 16)
    f32 = jnp.float32

    def w(k, shape, fan_in):
        return jax.random.normal(k, shape, f32) * fan_in ** -0.5

    def gain(k, shape):
        return 1.0 + 0.05 * jax.random.normal(k, shape, f32)

    return {
        "x": jax.random.normal(ks[0], (BATCH, SEQ, D_MODEL), f32),
        "p": jax.random.normal(ks[1], (DEPTH, BATCH, SEQ, D_PLE), f32),
        "w_in": w(ks[2], (DEPTH, D_MODEL, IN_COLS), D_MODEL),
        "w_sb_out": w(ks[3], (DEPTH, W_SB, D_MODEL), W_SB),
        "w_ca_out": w(ks[4], (DEPTH, W_CA, D_MODEL), W_CA),
        "w_mix_out": w(ks[5], (DEPTH, D_MODEL, D_MODEL), D_MODEL),
        "rel_bias": 0.3 * jax.random.normal(ks[6], (DEPTH, N_HEADS_CA, N_REL), f32),
        "g_mix": gain(ks[7], (DEPTH, D_MODEL)),
        "g_ffn": gain(ks[8], (DEPTH, D_MODEL)),
        "g_ple": gain(ks[9], (DEPTH, D_MODEL)),
        "g_final": gain(ks[10], (D_MODEL,)),
        "w_ffn_in": w(ks[11], (DEPTH, D_MODEL, 2 * D_FF), D_MODEL),
        "w_ffn_out": w(ks[12], (DEPTH, D_FF, D_MODEL), D_FF),
        "w_ple_in": w(ks[13], (DEPTH, D_PLE, D_MODEL), D_PLE),
        "w_ple_gate": w(ks[14], (DEPTH, D_MODEL, D_MODEL), D_MODEL),
    }


def reference(x, p, w_in, w_sb_out, w_ca_out, w_mix_out, rel_bias, g_mix, g_ffn, g_ple, g_final,
              w_ffn_in, w_ffn_out, w_ple_in, w_ple_gate):
    B, S, _ = x.shape
    splits = [W_SB, 2 * W_SB, 3 * W_SB, 3 * W_SB + W_CA, 3 * W_SB + 2 * W_CA, 3 * W_SB + 3 * W_CA,
              3 * W_SB + 3 * W_CA + D_MODEL]
    for i in range(DEPTH):
        h = rmsnorm(x, g_mix[i])
        proj = h @ w_in[i]
        q_sb, k_sb, v_sb, q_ca, k_ca, v_ca, gate_sb, gate_ca = jnp.split(proj, splits, axis=-1)
        y_sb = stick_breaking_attention(q_sb.reshape(B, S, N_HEADS_SB, HEAD_DIM),
                                        k_sb.reshape(B, S, N_HEADS_SB, HEAD_DIM),
                                        v_sb.reshape(B, S, N_HEADS_SB, HEAD_DIM)).reshape(B, S, W_SB)
        y_ca = chunked_relpos_attention(q_ca.reshape(B, S, N_HEADS_CA, HEAD_DIM),
                                        k_ca.reshape(B, S, N_HEADS_CA, HEAD_DIM),
                                        v_ca.reshape(B, S, N_HEADS_CA, HEAD_DIM),
                                        rel_bias[i]).reshape(B, S, W_CA)
        merged = (jax.nn.sigmoid(gate_sb) * (y_sb @ w_sb_out[i])
                  + jax.nn.sigmoid(gate_ca) * (y_ca @ w_ca_out[i]))
        x = x + merged @ w_mix_out[i]
        h = rmsnorm(x, g_ffn[i])
        g_ff, u_ff = jnp.split(h @ w_ffn_in[i], 2, axis=-1)
        x = x + (jax.nn.silu(g_ff) * u_ff) @ w_ffn_out[i]
        h = rmsnorm(x, g_ple[i])
        x = x + jax.nn.sigmoid(h @ w_ple_gate[i]) * (p[i] @ w_ple_in[i])
    return rmsnorm(x, g_final)
```

```python
import numpy as np
import concourse.bass as bass
import concourse.mybir as mybir
from concourse.bass_utils import run_bass_kernel_spmd

F32 = mybir.dt.float32
BF16 = mybir.dt.bfloat16
AF = mybir.ActivationFunctionType
ALU = mybir.AluOpType

D = 2048
S = 2048
DC = 16
NH = 8
HD = 128
DFF = 5632
FC = 44
EPS = 1e-6
SCALE = HD ** -0.5
NCORES = 8
NS = 6
NEGM = -30000.0
STRICT_SAME_ENGINE = True

U_QSB, U_KSB, U_VSB, U_QCA, U_KCA, U_VCA, U_GSB, U_GCA = 0, 8, 16, 24, 32, 40, 48, 64
U_OUTS = 80
U_MIX = 96
U_FFI = 112
U_FFO = 200
U_PG = 264
U_PI = 280
NU = 282


class Op:
    __slots__ = ("eng", "fn", "raw", "oth", "sig", "cnt", "dsem", "dval", "idx")

    def __init__(self, eng, fn):
        self.eng = eng
        self.fn = fn
        self.raw = []
        self.oth = []
        self.sig = False
        self.cnt = 0
        self.dsem = None
        self.dval = 0


class Prog:
    ENGS = ("pe", "act", "dve", "pool", "sp")

    def __init__(self):
        self.streams = {e: [] for e in self.ENGS}
        self.lastw = {}
        self.readers = {}
        self.dma_cnt = {}
        self.last_dma = {}

    def op(self, eng, fn, reads=(), writes=(), dsem=None, extra=()):
        o = Op(eng, fn)
        raw, oth = set(), set()
        for b in reads:
            w = self.lastw.get(b)
            if w is not None:
                raw.add(w)
        for b in writes:
            w = self.lastw.get(b)
            if w is not None:
                oth.add(w)
            for r in self.readers.get(b, {}).values():
                oth.add(r)
        for x in extra:
            raw.add(x)
        raw.discard(o)
        oth -= raw
        o.raw = list(raw)
        o.oth = list(oth)
        key = dsem if dsem is not None else eng
        for b in reads:
            self.readers.setdefault(b, {})[key] = o
        for b in writes:
            self.lastw[b] = o
            self.readers[b] = {}
        if dsem is not None:
            c = self.dma_cnt.get(dsem, 0) + 1
            self.dma_cnt[dsem] = c
            o.dsem = dsem
            o.dval = 16 * c
            self.last_dma[dsem] = o
        o.idx = len(self.streams[eng])
        self.streams[eng].append(o)
        return o

    def barrier(self, engs=("pe", "act", "dve", "sp")):
        lasts = []
        for e in ("pe", "act", "dve"):
            if self.streams[e]:
                lasts.append(self.streams[e][-1])
        for nm, o in self.last_dma.items():
            if not nm.startswith("w"):
                lasts.append(o)
        for e in engs:
            self.op(e, None, extra=lasts)

    def _needed(self, o):
        res = []
        for d in o.raw:
            if d.dsem is not None:
                res.append(d)
            elif d.eng == o.eng:
                if o.eng != "pe" and d.fn is not None:
                    res.append(d)
            else:
                res.append(d)
        for d in o.oth:
            if d.dsem is not None:
                res.append(d)
            elif d.eng != o.eng:
                res.append(d)
            elif STRICT_SAME_ENGINE and o.eng != "pe" and d.fn is not None:
                res.append(d)
        return res

    def finalize(self):
        for e in self.ENGS:
            for o in self.streams[e]:
                for d in self._needed(o):
                    if d.dsem is None:
                        d.sig = True
        for e in self.ENGS:
            cnt = 0
            for o in self.streams[e]:
                if o.dsem is None and o.sig and o.fn is not None:
                    cnt += 1
                o.cnt = cnt

    def emit(self, nc, block, engsem, dmasem):
        prog = self

        def run(ename, e):
            seen = {}
            for o in prog.streams[ename]:
                for d in prog._needed(o):
                    if d.dsem is not None:
                        sem, val = dmasem[d.dsem], d.dval
                    else:
                        sem, val = engsem[d.eng], d.cnt
                    if val <= 0:
                        continue
                    k = sem.num
                    if seen.get(k, 0) < val:
                        e.wait_ge(sem, val)
                        seen[k] = val
                if o.fn is not None:
                    ins = o.fn(e)
                    if o.dsem is not None:
                        ins.then_inc(dmasem[o.dsem], 16)
                    elif o.sig:
                        ins.then_inc(engsem[ename], 1)

        @block.tensor
        def _(e):
            run("pe", e)

        @block.scalar
        def _(e):
            run("act", e)

        @block.vector
        def _(e):
            run("dve", e)

        @block.gpsimd
        def _(e):
            run("pool", e)

        @block.sync
        def _(e):
            run("sp", e)


def build_program():
    nc = bass.Bass("TRN2", target_bir_lowering=False)
    P = Prog()

    xT = nc.dram_tensor("xT", [DC, 128, S], F32, kind="ExternalInput").ap()
    pT = nc.dram_tensor("pT", [2, 128, S], F32, kind="ExternalInput").ap()
    wall = nc.dram_tensor("wall", [NU, 128, 2048], F32, kind="ExternalInput").ap()
    cst = nc.dram_tensor("cst", [128, 2496], F32, kind="ExternalInput").ap()
    btd = nc.dram_tensor("bt", [NH, 128, 640], F32, kind="ExternalInput").ap()
    outT = nc.dram_tensor("outT", [DC, 128, S], F32, kind="ExternalOutput").ap()
    mscr = nc.dram_tensor("mscr", [DC, 128, S], BF16).ap()

    xT_v = xT.rearrange("c p t -> p c t")
    pT_v = pT.rearrange("c p t -> p c t")
    mscr_v = mscr.rearrange("c p t -> p c t")

    M0 = 20608
    cnt = [0]

    def mk(name, shape, dtype, off):
        cnt[0] += 1
        return nc.alloc_sbuf_tensor_at(f"{name}_{cnt[0]}", shape, dtype, offset=M0 + off)

    ones_bf = mk("ones", [128, 128], BF16, 0)
    U_bf = mk("utri", [128, 128], BF16, 256)
    masks = mk("masks", [128, 4, 512], BF16, 512)
    gains = mk("gains", [128, 64], F32, 4608)
    negones = mk("negones", [128, 128], BF16, 4864)
    CONST_SZ = 5120
    slots = [mk(f"slot{k}", [128, 2048], BF16, CONST_SZ + 4096 * k) for k in range(NS)]
    MAIN = CONST_SZ + 4096 * NS

    h1T = mk("h1T", [128, DC, S], BF16, MAIN)
    yT = mk("yT", [128, DC, S], BF16, MAIN + 65536)
    xin = [mk(f"xin{k}", [128, DC, 512], F32, MAIN + 65536 + 32768 * k) for k in range(2)]
    L = MAIN + 131072
    QT = [mk(f"QT{k}", [128, S], BF16, L + 4096 * k) for k in range(2)]
    KT = [mk(f"KT{k}", [128, S], BF16, L + 8192 + 4096 * k) for k in range(2)]
    Vt = [mk(f"Vt{k}", [128, 16, 128], BF16, L + 16384 + 4096 * k) for k in range(2)]
    T0 = L + 24576
    sqA = [mk(f"sqA{k}", [128, 4, 512], BF16, T0 + 4096 * k) for k in range(2)]
    lnvA = mk("lnvA", [128, 512], F32, T0 + 8192)
    rbufA = mk("rbufA", [128, 512], F32, T0 + 10240)
    cst_stage = mk("cststage", [128, 2496], F32, T0 + 12288)
    e_t = [mk(f"e{k}", [128, 512], F32, T0 + 2048 * k) for k in range(2)]
    at_t = [mk(f"at{k}", [128, 512], BF16, T0 + 2048 * k) for k in range(2)]
    spm_t = [mk(f"spm{k}", [128, 512], BF16, T0 + 4096 + 1024 * k) for k in range(2)]
    spb_t = [mk(f"spb{k}", [128, 512], BF16, T0 + 6144 + 1024 * k) for k in range(2)]
    ss_t = [mk(f"ss{k}", [128, 512], BF16, T0 + 8192 + 1024 * k) for k in range(2)]
    a_t = [mk(f"a{k}", [128, 512], BF16, T0 + 10240 + 1024 * k) for k in range(2)]
    z_t = [mk(f"z{k}", [128, 512], F32, T0 + 2048 * k) for k in range(2)]
    ca_t = [mk(f"ca{k}", [128, 512], BF16, T0 + 6144 + 1024 * k) for k in range(2)]
    rden = mk("rden", [128, 512], F32, T0 + 4096)
    BT = [mk(f"BT{k}", [128, 640], F32, T0 + 12288 + 2560 * k) for k in range(2)]
    assert T0 + 12288 + 9984 <= 208768
    sg = mk("sg", [128, S], F32, L)
    sgc = mk("sgc", [128, S], F32, L + 8192)
    mbuf = mk("mbuf", [128, S], F32, L + 16384)
    mgo = [mk(f"mgo{k}", [128, S], BF16, L + 24576 + 4096 * k) for k in range(2)]

    x1 = mk("x1", [128, DC, 1024], F32, MAIN)
    h2 = mk("h2", [128, DC, 1024], BF16, MAIN + 65536)
    act = mk("act", [128, 22, 1024], BF16, MAIN + 98304)
    mg = mk("mg", [128, DC, 1024], BF16, MAIN + 98304)
    L2 = MAIN + 143360
    sqB = [mk(f"sqB{k}", [128, 4, 512], BF16, L2 + 4096 * k) for k in range(2)]
    lnvB = mk("lnvB", [128, 512], F32, L2 + 8192)
    rbufB = mk("rbufB", [128, 512], F32, L2 + 10240)
    R12 = L2 + 12288
    xc = [mk(f"xc{k}", [128, 1024], F32, R12 + 4096 * k) for k in range(2)]
    pTf = mk("pTf", [128, 2, 1024], F32, R12)
    pTb = mk("pTb", [128, 2, 1024], BF16, R12 + 8192)
    rbufF = mk("rbufF", [128, 512], F32, L2 + 32768)
    SX = MAIN + 131072
    stg_loop = [(mk("stg0", [128, 1024], F32, R12 + 8192), ["r4", "r5"], "st0")] + \
               [(mk(f"stgx{i}", [128, 1024], F32, SX + 4096 * i), [f"sx{i}"], f"sx{i}") for i in range(3)]
    stg_tail = stg_loop + [(mk("stg1", [128, 1024], F32, R12), ["r0", "r1"], "st1"),
                           (mk("stg2", [128, 1024], F32, R12 + 4096), ["r2", "r3"], "st2")]
    sl_t = [mk(f"sl{k}", [128, 512], F32, L2 + 24576 + 2048 * k) for k in range(2)]
    tmp_t = [mk(f"tmp{k}", [128, 512], F32, L2 + 28672 + 2048 * k) for k in range(2)]
    upi = [mk(f"upi{k}", [128, 2048], BF16, MAIN + 131072 + 4096 * k) for k in range(2)]

    banks = [nc.alloc_psum_tensor(f"bk{i}", [128, 512], F32) for i in range(8)]

    def ACT(out, in_, func, reads, writes, bias=None, scale=None):
        def fn(e):
            kw = {}
            if bias is not None:
                kw["bias"] = bias
            if scale is not None:
                kw["scale"] = scale
            return e.activation(out=out, in_=in_, func=func, **kw)
        return P.op("act", fn, reads, writes)

    def TT(out, in0, in1, op, reads, writes):
        return P.op("dve", lambda e: e.tensor_tensor(out=out, in0=in0, in1=in1, op=op), reads, writes)

    def STT(out, in0, scalar, in1, op0, op1, reads, writes):
        return P.op("dve", lambda e: e.scalar_tensor_tensor(out=out, in0=in0, scalar=scalar, in1=in1,
                                                            op0=op0, op1=op1), reads, writes)

    def VCOPY(out, in_, reads, writes):
        return P.op("dve", lambda e: e.tensor_copy(out=out, in_=in_), reads, writes)

    def MM(out, pairs, reads, writes, start=True, stop=True, skip=False):
        def fn(e):
            n = len(pairs)
            ins = None
            for i, (l, r) in enumerate(pairs):
                kw = {}
                if skip:
                    kw["skip_group_check"] = True
                ins = e.matmul(out, l, r, start=(start and i == 0), stop=(stop and i == n - 1), **kw)
            return ins
        return P.op("pe", fn, reads, writes)

    def DMA(eng, out, in_, dsem, reads, writes):
        return P.op(eng, lambda e: e.dma_start(out=out, in_=in_), reads, writes, dsem=dsem)

    useq = [0]
    pinned = set()

    def load_unit(u, ncols=2048, pin=False):
        while useq[0] % NS in pinned:
            useq[0] += 1
        k = useq[0] % NS
        useq[0] += 1
        if pin:
            pinned.add(k)
        DMA("pool", slots[k][:, 0:ncols], wall[u][:, 0:ncols], f"w{k}", [], [f"slot{k}"])
        return slots[k], f"slot{k}"

    def unpin(name):
        pinned.discard(int(name[4:]))

    def bkn(i):
        return f"bk{i}"

    def cs(c):
        return slice(c * 128, (c + 1) * 128)

    def ts(t, w=512):
        return slice(t * w, (t + 1) * w)

    DMA("sp", cst_stage[:], cst[:, :], "cst", [], ["cststage"])
    ACT(ones_bf[:], cst_stage[:, 0:128], AF.Copy, ["cststage"], ["ones"])
    ACT(U_bf[:], cst_stage[:, 128:256], AF.Copy, ["cststage"], ["utri"])
    for r in range(4):
        ACT(masks[:, r, :], cst_stage[:, 256 + 512 * r:256 + 512 * (r + 1)], AF.Copy, ["cststage"], ["masks"])
    ACT(gains[:], cst_stage[:, 2304:2368], AF.Copy, ["cststage"], ["gains"])
    ACT(negones[:], cst_stage[:, 2368:2496], AF.Copy, ["cststage"], ["negones"])

    nbank = [0]

    def rms_block(src4, src1, srcnames, gidx, dst, dstnames, sq, lnv, rbuf, tag, qnames=None, cnames=None):
        bi = nbank[0] % 4
        nbank[0] += 1
        bank = banks[bi]
        for q in range(4):
            s = q % 2
            ACT(sq[s][:], src4(q), AF.Square, qnames(q) if qnames else srcnames, [f"sq{tag}{s}"])
            MM(bank[:], [(ones_bf[:], sq[s][:, i, :]) for i in range(4)], [f"sq{tag}{s}", "ones"], [bkn(bi)],
               start=(q == 0), stop=(q == 3))
        ACT(lnv[:], bank[:], AF.Ln, [bkn(bi)], [f"lnv{tag}"], bias=EPS, scale=1.0 / D)
        ACT(rbuf[:], lnv[:], AF.Exp, [f"lnv{tag}"], [f"rbuf{tag}"], scale=-0.5)
        for c in range(DC):
            STT(dst(c), src1(c), gains[:, gidx * 16 + c:gidx * 16 + c + 1], rbuf[:], ALU.mult, ALU.mult,
                list(cnames(c) if cnames else srcnames) + [f"rbuf{tag}", "gains"], dstnames(c))

    for tg in range(4):
        k = tg % 2
        DMA("sp", xin[k][:], xT_v[:, :, ts(tg)], f"xin{k}", [], [f"xin{k}"])
        rms_block(lambda q, k=k: xin[k][:, 4 * q:4 * q + 4, :], lambda c, k=k: xin[k][:, c, :], [f"xin{k}"], 0,
                  lambda c, tg=tg: h1T[:, c, ts(tg)], lambda c, tg=tg: [f"h1T.{tg}"], sqA, lnvA, rbufA, "A")

    pbank = [0]

    NQUANTA = 96

    def proj_quanta_tgmajor(uq, uk, uv, hp):
        sq_, sqn = load_unit(uq)
        sk_, skn = load_unit(uk)
        sv_, svn = load_unit(uv)
        for tg in range(4):
            bi = 4 + pbank[0] % 2
            pbank[0] += 1
            MM(banks[bi][:], [(sq_[:, cs(c)], h1T[:, c, ts(tg)]) for c in range(DC)], [sqn, f"h1T.{tg}"], [bkn(bi)])
            ACT(QT[hp][:, ts(tg)], banks[bi][:], AF.Copy, [bkn(bi)], [f"QT{hp}.{tg}"], scale=SCALE)
            bi = 4 + pbank[0] % 2
            pbank[0] += 1
            MM(banks[bi][:], [(sk_[:, cs(c)], h1T[:, c, ts(tg)]) for c in range(DC)], [skn, f"h1T.{tg}"], [bkn(bi)])
            VCOPY(KT[hp][:, ts(tg)], banks[bi][:], [bkn(bi)], [f"KT{hp}.{tg}"])
            bi = 4 + pbank[0] % 2
            pbank[0] += 1
            tq = tg
            for i in range(4):
                tt = tq * 4 + i
                MM(banks[bi][:, cs(i)], [(h1T[:, c, cs(tt)], sv_[:, cs(c)]) for c in range(DC)],
                   [svn, f"h1T.{tq}"], [bkn(bi)])
            P.op("act", lambda e, bi=bi, tq=tq: e.activation(
                out=Vt[hp][:, 4 * tq:4 * tq + 4, :], in_=banks[bi][:].rearrange("p (a b) -> p a b", a=4),
                func=AF.Copy), [bkn(bi)], [f"V{hp}.{tq}"])

    def proj_quanta(uq, uk, uv, hp):
        sq_, sqn = load_unit(uq)
        for tg in range(4):
            bi = 4 + pbank[0] % 2
            pbank[0] += 1
            for q8 in range(8):
                MM(banks[bi][:], [(sq_[:, cs(c)], h1T[:, c, ts(tg)]) for c in range(2 * q8, 2 * q8 + 2)],
                   [sqn, f"h1T.{tg}"], [bkn(bi)], start=(q8 == 0), stop=(q8 == 7))
                if q8 == 7:
                    ACT(QT[hp][:, ts(tg)], banks[bi][:], AF.Copy, [bkn(bi)], [f"QT{hp}.{tg}"], scale=SCALE)
                yield
        sk_, skn = load_unit(uk)
        for tg in range(4):
            bi = 4 + pbank[0] % 2
            pbank[0] += 1
            for q8 in range(8):
                MM(banks[bi][:], [(sk_[:, cs(c)], h1T[:, c, ts(tg)]) for c in range(2 * q8, 2 * q8 + 2)],
                   [skn, f"h1T.{tg}"], [bkn(bi)], start=(q8 == 0), stop=(q8 == 7))
                if q8 == 7:
                    VCOPY(KT[hp][:, ts(tg)], banks[bi][:], [bkn(bi)], [f"KT{hp}.{tg}"])
                yield
        sv_, svn = load_unit(uv)
        for tq in range(4):
            bi = 4 + pbank[0] % 2
            pbank[0] += 1
            for i in range(4):
                tt = tq * 4 + i
                for q4 in range(2):
                    MM(banks[bi][:, cs(i)], [(h1T[:, c, cs(tt)], sv_[:, cs(c)]) for c in range(8 * q4, 8 * q4 + 8)],
                       [svn, f"h1T.{tq}"], [bkn(bi)], start=(q4 == 0), stop=(q4 == 1))
                    if i == 3 and q4 == 1:
                        P.op("act", lambda e, bi=bi, tq=tq: e.activation(
                            out=Vt[hp][:, 4 * tq:4 * tq + 4, :], in_=banks[bi][:].rearrange("p (a b) -> p a b", a=4),
                            func=AF.Copy), [bkn(bi)], [f"V{hp}.{tq}"])
                    yield

    class Puller:
        def __init__(self, gen, points):
            self.gen = gen
            self.left = NQUANTA if gen is not None else 0
            self.points = points

        def pull(self):
            if self.gen is None or self.left <= 0:
                self.points -= 1
                return
            k = -(-self.left // max(self.points, 1))
            self.points -= 1
            for _ in range(k):
                try:
                    next(self.gen)
                    self.left -= 1
                except StopIteration:
                    self.left = 0
                    return

        def flush(self):
            if self.gen is None:
                return
            for _ in self.gen:
                pass

    def pull(gen, n):
        if gen is None:
            return
        for _ in range(n):
            try:
                next(gen)
            except StopIteration:
                return

    def sb_attention(h, hp, pl):
        Qh, Kh, Vh = QT[hp], KT[hp], Vt[hp]
        pairs = []
        for g in range(4):
            bl = list(range(4 * g + 3, -1, -1))
            for ig, b in enumerate(bl):
                pairs.append((g, b, ig, len(bl)))
        n = len(pairs)
        for it in range(n + 2):
            if it < n:
                i = it
                g, b, ig, ng = pairs[i]
                k = i % 2
                r = b - 4 * g
                c0 = 128 * r if r > 0 else 0
                if ig == 0:
                    P.op("dve", lambda e, yb=6 + g % 2: e.memset(banks[yb][:], 0.0), [], [bkn(6 + g % 2)])
                MM(banks[k][:, c0:], [(Kh[:, cs(b)], Qh[:, g * 512 + c0:(g + 1) * 512])],
                   [f"KT{hp}.{b // 4}", f"QT{hp}.{g}"], [bkn(k)])
                ACT(e_t[k][:, c0:], banks[k][:, c0:], AF.Exp, [bkn(k)], [f"e{k}"])
                if r >= 0:
                    if c0 > 0:
                        P.op("dve", lambda e, k=k, c0=c0: e.memset(spb_t[k][:, 0:c0], 0.0), [], [f"spb{k}"])
                    ACT(spm_t[k][:, c0:], e_t[k][:, c0:], AF.Ln, [f"e{k}"], [f"spm{k}"], bias=1.0)
                    TT(spb_t[k][:, c0:], spm_t[k][:, c0:], masks[:, r, c0:], ALU.mult, [f"spm{k}", "masks"],
                       [f"spb{k}"])
                else:
                    ACT(spb_t[k][:], e_t[k][:], AF.Ln, [f"e{k}"], [f"spb{k}"], bias=1.0)
            pl.pull()
            if 0 <= it - 1 < n:
                i = it - 1
                g, b, ig, ng = pairs[i]
                k = i % 2
                r = b - 4 * g
                c0 = 128 * r if r > 0 else 0
                prs = [(Kh[:, cs(b)], Qh[:, g * 512 + c0:(g + 1) * 512]), (U_bf[:], spb_t[k][:, c0:])]
                rd = [f"KT{hp}.{b // 4}", f"QT{hp}.{g}", "utri", f"spb{k}"]
                if ig >= 1:
                    prs.append((negones[:], ss_t[k][:, c0:]))
                    rd += ["negones", f"ss{k}"]
                MM(banks[2 + k][:, c0:], prs, rd, [bkn(2 + k)])
                if ig + 1 < ng:
                    if ig == 0:
                        VCOPY(ss_t[1 - k][:], spb_t[k][:], [f"spb{k}"], [f"ss{1 - k}"])
                    else:
                        TT(ss_t[1 - k][:], ss_t[k][:], spb_t[k][:], ALU.add, [f"ss{k}", f"spb{k}"], [f"ss{1 - k}"])
                if r >= 0:
                    ACT(at_t[k][:, c0:], banks[2 + k][:, c0:], AF.Exp, [bkn(2 + k)], [f"e{k}"])
                    TT(a_t[k][:, c0:], at_t[k][:, c0:], masks[:, r, c0:], ALU.mult, [f"e{k}", "masks"], [f"a{k}"])
                else:
                    ACT(a_t[k][:], banks[2 + k][:], AF.Exp, [bkn(2 + k)], [f"a{k}"])
            pl.pull()
            if 0 <= it - 2 < n:
                i = it - 2
                g, b, ig, ng = pairs[i]
                k = i % 2
                r = b - 4 * g
                c0 = 128 * r if r > 0 else 0
                ybi = 6 + g % 2
                MM(banks[ybi][:, c0:], [(Vh[:, b, :], a_t[k][:, c0:])], [f"V{hp}.{b // 4}", f"a{k}"], [bkn(ybi)],
                   start=False, stop=(ig == ng - 1), skip=True)
                if ig == ng - 1:
                    ACT(yT[:, h, ts(g)], banks[ybi][:], AF.Copy, [bkn(ybi)], [f"yT.{g}"])

    def ca_attention(h, hp, btk, pl):
        Qh, Kh, Vh = QT[hp], KT[hp], Vt[hp]
        pairs = []
        for g in range(4):
            first = 4 * g - 1 if g >= 1 else 0
            kts = [first] + [kt for kt in range(max(0, 4 * g - 4), 4 * g + 4) if kt != first]
            for ig, kt in enumerate(kts):
                pairs.append((g, kt, ig, len(kts)))
        n = len(pairs)

        def rng(g, kt):
            lo = max(4 * g, kt)
            hi = min(4 * g + 3, kt + 4)
            c0 = (lo - 4 * g) * 128
            c1 = (hi - 4 * g + 1) * 128
            j0 = (lo - kt) * 128
            return c0, c1, j0

        def finalize(g):
            nb, db = 2 + (g % 2), 6 + (g % 2)
            P.op("dve", lambda e, db=db: e.reciprocal(out=rden[:], in_=banks[db][:]), [bkn(db)], ["spm0", "spm1"])
            TT(yT[:, 8 + h, ts(g)], banks[nb][:], rden[:], ALU.mult, [bkn(nb), "spm0", "spm1"], [f"yT.{g}"])

        pending = []
        for it in range(n + 1):
            if it < n:
                i = it
                g, kt, ig, ng = pairs[i]
                k = i % 2
                c0, c1, j0 = rng(g, kt)
                MM(banks[k][:, c0:c1], [(Kh[:, cs(kt)], Qh[:, g * 512 + c0:g * 512 + c1])],
                   [f"KT{hp}.{kt // 4}", f"QT{hp}.{g}"], [bkn(k)])
                TT(z_t[k][:, c0:c1], banks[k][:, c0:c1], BT[btk][:, j0:j0 + (c1 - c0)], ALU.add,
                   [bkn(k), f"BT{btk}"], [f"e{k}"])
                ACT(ca_t[k][:, c0:c1], z_t[k][:, c0:c1], AF.Exp, [f"e{k}"], [f"spb{k}"])
            pl.pull()
            if 0 <= it - 1 < n:
                i = it - 1
                g, kt, ig, ng = pairs[i]
                k = i % 2
                nb, db = 2 + (g % 2), 6 + (g % 2)
                c0, c1, j0 = rng(g, kt)
                MM(banks[nb][:, c0:c1], [(Vh[:, kt, :], ca_t[k][:, c0:c1])], [f"V{hp}.{kt // 4}", f"spb{k}"],
                   [bkn(nb)], start=(ig == 0), stop=(ig == ng - 1), skip=True)
                MM(banks[db][:, c0:c1], [(ones_bf[:], ca_t[k][:, c0:c1])], ["ones", f"spb{k}"], [bkn(db)],
                   start=(ig == 0), stop=(ig == ng - 1), skip=True)
                if ig == ng - 1:
                    pending.append((g, it + 2))
            while pending and pending[0][1] <= it:
                finalize(pending.pop(0)[0])
        for g, _ in pending:
            finalize(g)

    heads = [("sb", h) for h in range(NH)] + [("ca", h) for h in range(NH)]

    def make_gen(idx):
        kind, h = heads[idx]
        if kind == "sb":
            return proj_quanta(U_QSB + h, U_KSB + h, U_VSB + h, idx % 2)
        DMA("sp", BT[h % 2][:], btd[h], f"bt{h % 2}", [], [f"BT{h % 2}"])
        return proj_quanta(U_QCA + h, U_KCA + h, U_VCA + h, idx % 2)

    proj_quanta_tgmajor(U_QSB, U_KSB, U_VSB, 0)
    SB_POINTS = 2 * (40 + 2)
    CA_POINTS = 28 + 1
    for idx, (kind, h) in enumerate(heads):
        gen = make_gen(idx + 1) if idx + 1 < len(heads) else None
        if kind == "sb":
            pl = Puller(gen, SB_POINTS)
            sb_attention(h, idx % 2, pl)
        else:
            pl = Puller(gen, CA_POINTS)
            ca_attention(h, idx % 2, h % 2, pl)
        pl.flush()

    for j in range(DC):
        ug, ugn = load_unit(U_GSB + j)
        uo, uon = load_unit(U_OUTS + j)
        uc, ucn = load_unit(U_GCA + j)
        for tg in range(4):
            MM(banks[tg][:], [(ug[:, cs(c)], h1T[:, c, ts(tg)]) for c in range(DC)], [ugn, f"h1T.{tg}"], [bkn(tg)])
            ACT(sg[:, ts(tg)], banks[tg][:], AF.Sigmoid, [bkn(tg)], [f"sg.{tg}"])
        for tg in range(4):
            MM(banks[4 + tg][:], [(uo[:, cs(c)], yT[:, c, ts(tg)]) for c in range(8)], [uon, f"yT.{tg}"], [bkn(4 + tg)])
            TT(mbuf[:, ts(tg)], sg[:, ts(tg)], banks[4 + tg][:], ALU.mult, [f"sg.{tg}", bkn(4 + tg)], [f"mbuf.{tg}"])
        for tg in range(4):
            MM(banks[tg][:], [(uc[:, cs(c)], h1T[:, c, ts(tg)]) for c in range(DC)], [ucn, f"h1T.{tg}"], [bkn(tg)])
            ACT(sgc[:, ts(tg)], banks[tg][:], AF.Sigmoid, [bkn(tg)], [f"sgc.{tg}"])
        for tg in range(4):
            MM(banks[4 + tg][:], [(uo[:, cs(8 + c)], yT[:, 8 + c, ts(tg)]) for c in range(8)], [uon, f"yT.{tg}"],
               [bkn(4 + tg)])
            TT(sgc[:, ts(tg)], sgc[:, ts(tg)], banks[4 + tg][:], ALU.mult, [f"sgc.{tg}", bkn(4 + tg)], [f"sgc.{tg}"])
            TT(mgo[j % 2][:, ts(tg)], mbuf[:, ts(tg)], sgc[:, ts(tg)], ALU.add, [f"mbuf.{tg}", f"sgc.{tg}"],
               [f"mgo{j % 2}"])
        DMA("sp", mscr[j], mgo[j % 2][:], f"mst{j % 2}", [f"mgo{j % 2}"], [f"M.{j}"])
    P.barrier()

    def r12(lo, hi):
        return [f"r{i}" for i in range(lo, hi)]

    def x1n(j, t2):
        return f"x1.{j}.{t2}"

    rF = [rbufB, rbufF]
    rFn = ["rbufB", "rbufF"]
    oi = [0]

    def final_out(hb, j, tail=False):
        lst = stg_tail if tail else stg_loop
        buf, nms, ds = lst[oi[0] % len(lst)]
        oi[0] += 1
        for t2 in range(2):
            STT(buf[:, ts(t2)], x1[:, j, ts(t2)], gains[:, 48 + j:49 + j], rF[t2][:], ALU.mult, ALU.mult,
                [x1n(j, t2), rFn[t2], "gains"], nms)
        DMA("sp", outT[j][:, hb * 1024:(hb + 1) * 1024], buf[:], ds, nms, [])

    sli = 0
    sqi = [0]

    def stat_act(j, t2):
        q_ = sqi[0] % 2
        sqi[0] += 1
        ACT(sqB[q_][:, 0, :], x1[:, j, ts(t2)], AF.Square, [x1n(j, t2)], [f"sqB{q_}"])
        return q_

    def stat_pe(j, t2, q_):
        MM(banks[6 + t2][:], [(ones_bf[:], sqB[q_][:, 0, :])], [f"sqB{q_}", "ones"], [bkn(6 + t2)],
           start=(j == 0), stop=(j == DC - 1))

    class Stats:
        def __init__(self):
            self.pend = None
            self.q = None

        def begin(self):
            if self.pend is not None:
                self.q = stat_act(*self.pend)

        def mid(self):
            if self.pend is not None:
                stat_pe(*self.pend, self.q)

        def end(self, j, t2):
            self.pend = (j, t2)

        def finish(self):
            stat_pe(*self.pend, stat_act(*self.pend))
            self.pend = None
            for t2 in range(2):
                ACT(lnvB[:], banks[6 + t2][:], AF.Ln, [bkn(6 + t2)], ["lnvB"], bias=EPS, scale=1.0 / D)
                ACT(rF[t2][:], lnvB[:], AF.Exp, ["lnvB"], [rFn[t2]], scale=-0.5)

    def apply_norm(gidx):
        for t2 in range(2):
            for c in range(DC):
                STT(h2[:, c, ts(t2)], x1[:, c, ts(t2)], gains[:, gidx * 16 + c:gidx * 16 + c + 1], rF[t2][:],
                    ALU.mult, ALU.mult, [x1n(c, t2), rFn[t2], "gains"], [f"h2.{c}.{t2}"])

    def h2n(t2):
        return [f"h2.{c}.{t2}" for c in range(DC)]

    def mm_first(bank_i, wt, wname, t2):
        for c in range(DC):
            MM(banks[bank_i][:], [(wt[:, cs(c)], h2[:, c, ts(t2)])], [wname, f"h2.{c}.{t2}"], [bkn(bank_i)],
               start=(c == 0), stop=(c == DC - 1))

    def t2_first_order(n_items, lead=3):
        steps = [(i, 0) for i in range(lead)] + [(i, 1) for i in range(lead)]
        for i in range(lead, n_items):
            steps += [(i, 0), (i, 1)]
        return steps

    for hb in range(2):
        hsl = slice(hb * 1024, (hb + 1) * 1024)
        DMA("sp", mg[:], mscr_v[:, :, hsl], "mgld", [f"M.{j}" for j in range(DC)], ["actreg"])
        st = Stats()
        DMA("sp", xc[0][:], xT[0][:, hsl], "xc0", [], r12(0, 2))
        for j in range(DC):
            if j + 1 < DC:
                xn = (j + 1) % 2
                DMA("sp", xc[xn][:], xT[j + 1][:, hsl], f"xc{xn}", [], r12(2 * xn, 2 * xn + 2))
            if hb > 0:
                final_out(hb - 1, j)
            um, umn = load_unit(U_MIX + j)
            xk = j % 2
            for t2 in range(2):
                bi = (2 * j + t2) % 6
                st.begin()
                MM(banks[bi][:], [(um[:, cs(c)], mg[:, c, ts(t2)]) for c in range(DC)], [umn, "actreg"], [bkn(bi)])
                st.mid()
                TT(x1[:, j, ts(t2)], xc[xk][:, ts(t2)], banks[bi][:], ALU.add, r12(2 * xk, 2 * xk + 2) + [bkn(bi)],
                   [x1n(j, t2)])
                st.end(j, t2)
        st.finish()
        apply_norm(1)
        for hf in range(2):
            units = {}
            steps = t2_first_order(22) if hf == 0 else [(jj, t2) for jj in range(22) for t2 in range(2)]
            for jj, t2 in steps:
                j = hf * 22 + jj
                if jj not in units:
                    units[jj] = (load_unit(U_FFI + 2 * j), load_unit(U_FFI + 2 * j + 1))
                (ugj, ugn), (uuj, uun) = units[jj]
                bg = (jj % 2) * 4 + t2
                bu = (jj % 2) * 4 + 2 + t2
                if hf == 0 and (jj, t2) == (0, 0):
                    for c in range(DC):
                        MM(banks[bg][:], [(ugj[:, cs(c)], h2[:, c, ts(t2)])], [ugn, f"h2.{c}.{t2}"], [bkn(bg)],
                           start=(c == 0), stop=(c == DC - 1))
                        MM(banks[bu][:], [(uuj[:, cs(c)], h2[:, c, ts(t2)])], [uun, f"h2.{c}.{t2}"], [bkn(bu)],
                           start=(c == 0), stop=(c == DC - 1))
                else:
                    MM(banks[bg][:], [(ugj[:, cs(c)], h2[:, c, ts(t2)]) for c in range(DC)], [ugn] + h2n(t2), [bkn(bg)])
                    MM(banks[bu][:], [(uuj[:, cs(c)], h2[:, c, ts(t2)]) for c in range(DC)], [uun] + h2n(t2), [bkn(bu)])
                s_ = sli % 2
                sli += 1
                ACT(sl_t[s_][:], banks[bg][:], AF.Silu, [bkn(bg)], [f"sl{s_}"])
                wr = ["actreg"] + (["sx0", "sx1", "sx2"] if jj >= 16 else [])
                TT(act[:, jj, ts(t2)], sl_t[s_][:], banks[bu][:], ALU.mult, [f"sl{s_}", bkn(bu)], wr)
            st = Stats() if hf == 1 else None
            for j in range(DC):
                u0, u0n = load_unit(U_FFO + (hf * 16 + j) * 2, 1408)
                u1, u1n = load_unit(U_FFO + (hf * 16 + j) * 2 + 1, 1408)
                for t2 in range(2):
                    bi = (2 * j + t2) % 6
                    prs = [(u0[:, cs(c)], act[:, c, ts(t2)]) for c in range(11)] + \
                          [(u1[:, cs(c)], act[:, 11 + c, ts(t2)]) for c in range(11)]
                    if st:
                        st.begin()
                    MM(banks[bi][:], prs, [u0n, u1n, "actreg", "sx0", "sx1", "sx2"], [bkn(bi)])
                    if st:
                        st.mid()
                    TT(x1[:, j, ts(t2)], x1[:, j, ts(t2)], banks[bi][:], ALU.add, [x1n(j, t2), bkn(bi)], [x1n(j, t2)])
                    if st:
                        st.end(j, t2)
            if st:
                st.finish()
        apply_norm(2)
        DMA("sp", pTf[:], pT_v[:, :, hsl], "pT", [], r12(0, 4))
        for c in range(2):
            ACT(pTb[:, c, :], pTf[:, c, :], AF.Copy, r12(0, 4), r12(4, 6))
        st = Stats()
        units = {}
        pis = {}
        for j, t2 in t2_first_order(DC):
            if j // 8 not in pis:
                if j // 8 == 1:
                    unpin(pis[0][1])
                pis[j // 8] = load_unit(U_PI + j // 8, pin=True)
            if j not in units:
                units[j] = load_unit(U_PG + j)
            ug, ugn = units[j]
            upk, upkn = pis[j // 8]
            bg = (j % 2) * 2 + t2
            bp = 4 + t2
            st.begin()
            if (j, t2) == (0, 0):
                mm_first(bg, ug, ugn, t2)
            else:
                MM(banks[bg][:], [(ug[:, cs(c)], h2[:, c, ts(t2)]) for c in range(DC)], [ugn] + h2n(t2), [bkn(bg)])
            MM(banks[bp][:], [(upk[:, cs((j % 8) * 2 + c)], pTb[:, c, ts(t2)]) for c in range(2)],
               [upkn] + r12(4, 6), [bkn(bp)])
            st.mid()
            s_ = sli % 2
            sli += 1
            ACT(sl_t[s_][:], banks[bg][:], AF.Sigmoid, [bkn(bg)], [f"sl{s_}"])
            TT(tmp_t[s_][:], sl_t[s_][:], banks[bp][:], ALU.mult, [f"sl{s_}", bkn(bp)], [f"tmp{s_}"])
            TT(x1[:, j, ts(t2)], x1[:, j, ts(t2)], tmp_t[s_][:], ALU.add, [x1n(j, t2), f"tmp{s_}"], [x1n(j, t2)])
            st.end(j, t2)
        st.finish()
        unpin(pis[1][1])
    for j in range(DC):
        final_out(1, j, tail=True)

    P.op("sp", None, extra=[o for o in P.last_dma.values()])

    P.finalize()
    engsem = {e: nc.alloc_semaphore(name=f"sem_{e}") for e in ("pe", "act", "dve", "pool")}
    dmasem = {nm: nc.alloc_semaphore(name=f"dsem_{nm}") for nm in P.dma_cnt}
    with nc.Block() as block:
        P.emit(nc, block, engsem, dmasem)
    return nc


def _unit(wsub):
    n = wsub.shape[0] // 128
    u = np.zeros((128, 2048), np.float32)
    u[:, :n * 128] = wsub.reshape(n, 128, 128).transpose(1, 0, 2).reshape(128, n * 128)
    return u


def _build_wall(w_in, w_sb_out, w_ca_out, w_mix_out, w_ffn_in, w_ffn_out, w_ple_in, w_ple_gate):
    wall = np.zeros((NU, 128, 2048), np.float32)
    for j in range(80):
        wall[j] = _unit(w_in[:, j * 128:(j + 1) * 128])
    outs = np.concatenate([w_sb_out, w_ca_out], axis=0)
    for j in range(16):
        wall[U_OUTS + j] = _unit(outs[:, j * 128:(j + 1) * 128])
        wall[U_MIX + j] = _unit(w_mix_out[:, j * 128:(j + 1) * 128])
        wall[U_PG + j] = _unit(w_ple_gate[:, j * 128:(j + 1) * 128])
    for j in range(FC):
        wall[U_FFI + 2 * j] = _unit(w_ffn_in[:, j * 128:(j + 1) * 128])
        wall[U_FFI + 2 * j + 1] = _unit(w_ffn_in[:, DFF + j * 128:DFF + (j + 1) * 128])
    for hf in range(2):
        for j in range(16):
            for part in range(2):
                r0 = (hf * 22 + part * 11) * 128
                wall[U_FFO + (hf * 16 + j) * 2 + part] = _unit(w_ffn_out[r0:r0 + 11 * 128, j * 128:(j + 1) * 128])
    for half in range(2):
        u = np.zeros((128, 2048), np.float32)
        for jj in range(8):
            j = half * 8 + jj
            blk = w_ple_in[:, j * 128:(j + 1) * 128]
            u[:, jj * 256:(jj + 1) * 256] = blk.reshape(2, 128, 128).transpose(1, 0, 2).reshape(128, 256)
        wall[U_PI + half] = u
    return wall


def _build_consts(g_mix, g_ffn, g_ple, g_final, rel_bias):
    cst = np.zeros((128, 2496), np.float32)
    cst[:, 2368:2496] = -1.0
    cst[:, 0:128] = 1.0
    jj = np.arange(128)[:, None]
    ss = np.arange(128)[None, :]
    cst[:, 128:256] = -(jj >= ss).astype(np.float32)
    p = np.arange(128)[:, None]
    t = np.arange(512)[None, :]
    for r in range(4):
        cst[:, 256 + 512 * r:256 + 512 * (r + 1)] = (t > 128 * r + p).astype(np.float32)
    for i, g in enumerate((g_mix, g_ffn, g_ple, g_final)):
        cst[:, 2304 + 16 * i:2304 + 16 * (i + 1)] = np.asarray(g, np.float32).reshape(16, 128).T
    pj = np.arange(128)[:, None]
    jq = np.arange(640)[None, :]
    idx = np.clip(pj - jq, -128, 63) + 128
    dchunk = (jq // 64) - (pj // 64)
    valid = (dchunk >= 0) & (dchunk <= 8)
    bt = np.where(valid[None], rel_bias[:, idx], np.float32(NEGM)).astype(np.float32)
    return cst, bt


_NC_CACHE = {}


def kernel(x, p, w_in, w_sb_out, w_ca_out, w_mix_out, rel_bias, g_mix, g_ffn, g_ple, g_final,
           w_ffn_in, w_ffn_out, w_ple_in, w_ple_gate):
    f = lambda a: np.asarray(a, dtype=np.float32)
    x = f(x)
    p = f(p)
    wall = _build_wall(f(w_in)[0], f(w_sb_out)[0], f(w_ca_out)[0], f(w_mix_out)[0], f(w_ffn_in)[0],
                       f(w_ffn_out)[0], f(w_ple_in)[0], f(w_ple_gate)[0])
    cst, bt = _build_consts(f(g_mix)[0], f(g_ffn)[0], f(g_ple)[0], f(g_final), f(rel_bias)[0])
    if "nc" not in _NC_CACHE:
        _NC_CACHE["nc"] = build_program()
    nc = _NC_CACHE["nc"]
    in_maps = []
    for b in range(NCORES):
        xTb = np.ascontiguousarray(x[b].T).reshape(DC, 128, S)
        pTb = np.ascontiguousarray(p[0, b].T).reshape(2, 128, S)
        in_maps.append({"xT": xTb, "pT": pTb, "wall": wall, "cst": cst, "bt": bt})
    res = run_bass_kernel_spmd(nc, in_maps, core_ids=list(range(NCORES)))
    out = np.empty((NCORES, S, D), np.float32)
    for b in range(NCORES):
        out[b] = res.results[b]["outT"].reshape(D, S).T
    return out
```

```python
import numpy as np
import concourse.bass as bass
import concourse.mybir as mybir
from concourse.bass_utils import run_bass_kernel_spmd

F32 = mybir.dt.float32
BF16 = mybir.dt.bfloat16
AF = mybir.ActivationFunctionType
ALU = mybir.AluOpType

D = 2048
S = 2048
DC = 16
NH = 8
HD = 128
DFF = 5632
FC = 44
EPS = 1e-6
SCALE = HD ** -0.5
NCORES = 8
NS = 6
NEGM = -30000.0
STRICT_SAME_ENGINE = True

U_QSB, U_KSB, U_VSB, U_QCA, U_KCA, U_VCA, U_GSB, U_GCA = 0, 8, 16, 24, 32, 40, 48, 64
U_OUTS = 80
U_MIX = 96
U_FFI = 112
U_FFO = 200
U_PG = 264
U_PI = 280
NU = 282


class Op:
    __slots__ = ("eng", "fn", "raw", "oth", "sig", "cnt", "dsem", "dval", "idx")

    def __init__(self, eng, fn):
        self.eng = eng
        self.fn = fn
        self.raw = []
        self.oth = []
        self.sig = False
        self.cnt = 0
        self.dsem = None
        self.dval = 0


class Prog:
    ENGS = ("pe", "act", "dve", "pool", "sp")

    def __init__(self):
        self.streams = {e: [] for e in self.ENGS}
        self.lastw = {}
        self.readers = {}
        self.dma_cnt = {}
        self.last_dma = {}

    def op(self, eng, fn, reads=(), writes=(), dsem=None, extra=()):
        o = Op(eng, fn)
        raw, oth = set(), set()
        for b in reads:
            w = self.lastw.get(b)
            if w is not None:
                raw.add(w)
        for b in writes:
            w = self.lastw.get(b)
            if w is not None:
                oth.add(w)
            for r in self.readers.get(b, {}).values():
                oth.add(r)
        for x in extra:
            raw.add(x)
        raw.discard(o)
        oth -= raw
        o.raw = list(raw)
        o.oth = list(oth)
        key = dsem if dsem is not None else eng
        for b in reads:
            self.readers.setdefault(b, {})[key] = o
        for b in writes:
            self.lastw[b] = o
            self.readers[b] = {}
        if dsem is not None:
            c = self.dma_cnt.get(dsem, 0) + 1
            self.dma_cnt[dsem] = c
            o.dsem = dsem
            o.dval = 16 * c
            self.last_dma[dsem] = o
        o.idx = len(self.streams[eng])
        self.streams[eng].append(o)
        return o

    def barrier(self, engs=("pe", "act", "dve", "sp")):
        lasts = []
        for e in ("pe", "act", "dve"):
            if self.streams[e]:
                lasts.append(self.streams[e][-1])
        for nm, o in self.last_dma.items():
            if not nm.startswith("w"):
                lasts.append(o)
        for e in engs:
            self.op(e, None, extra=lasts)

    def _needed(self, o):
        res = []
        for d in o.raw:
            if d.dsem is not None:
                res.append(d)
            elif d.eng == o.eng:
                if o.eng != "pe" and d.fn is not None:
                    res.append(d)
            else:
                res.append(d)
        for d in o.oth:
            if d.dsem is not None:
                res.append(d)
            elif d.eng != o.eng:
                res.append(d)
            elif STRICT_SAME_ENGINE and o.eng != "pe" and d.fn is not None:
                res.append(d)
        return res

    def finalize(self):
        for e in self.ENGS:
            for o in self.streams[e]:
                for d in self._needed(o):
                    if d.dsem is None:
                        d.sig = True
        for e in self.ENGS:
            cnt = 0
            for o in self.streams[e]:
                if o.dsem is None and o.sig and o.fn is not None:
                    cnt += 1
                o.cnt = cnt

    def emit(self, nc, block, engsem, dmasem):
        prog = self

        def run(ename, e):
            seen = {}
            for o in prog.streams[ename]:
                for d in prog._needed(o):
                    if d.dsem is not None:
                        sem, val = dmasem[d.dsem], d.dval
                    else:
                        sem, val = engsem[d.eng], d.cnt
                    if val <= 0:
                        continue
                    k = sem.num
                    if seen.get(k, 0) < val:
                        e.wait_ge(sem, val)
                        seen[k] = val
                if o.fn is not None:
                    ins = o.fn(e)
                    if o.dsem is not None:
                        ins.then_inc(dmasem[o.dsem], 16)
                    elif o.sig:
                        ins.then_inc(engsem[ename], 1)

        @block.tensor
        def _(e):
            run("pe", e)

        @block.scalar
        def _(e):
            run("act", e)

        @block.vector
        def _(e):
            run("dve", e)

        @block.gpsimd
        def _(e):
            run("pool", e)

        @block.sync
        def _(e):
            run("sp", e)


def build_program():
    nc = bass.Bass("TRN2", target_bir_lowering=False)
    P = Prog()

    xT = nc.dram_tensor("xT", [DC, 128, S], F32, kind="ExternalInput").ap()
    pT = nc.dram_tensor("pT", [2, 128, S], F32, kind="ExternalInput").ap()
    wall = nc.dram_tensor("wall", [NU, 128, 2048], F32, kind="ExternalInput").ap()
    cst = nc.dram_tensor("cst", [128, 2496], F32, kind="ExternalInput").ap()
    btd = nc.dram_tensor("bt", [NH, 128, 640], F32, kind="ExternalInput").ap()
    outT = nc.dram_tensor("outT", [DC, 128, S], F32, kind="ExternalOutput").ap()
    mscr = nc.dram_tensor("mscr", [DC, 128, S], BF16).ap()

    xT_v = xT.rearrange("c p t -> p c t")
    pT_v = pT.rearrange("c p t -> p c t")
    mscr_v = mscr.rearrange("c p t -> p c t")

    M0 = 20608
    cnt = [0]

    def mk(name, shape, dtype, off):
        cnt[0] += 1
        return nc.alloc_sbuf_tensor_at(f"{name}_{cnt[0]}", shape, dtype, offset=M0 + off)

    ones_bf = mk("ones", [128, 128], BF16, 0)
    U_bf = mk("utri", [128, 128], BF16, 256)
    masks = mk("masks", [128, 4, 512], BF16, 512)
    gains = mk("gains", [128, 64], F32, 4608)
    negones = mk("negones", [128, 128], BF16, 4864)
    CONST_SZ = 5120
    slots = [mk(f"slot{k}", [128, 2048], BF16, CONST_SZ + 4096 * k) for k in range(NS)]
    MAIN = CONST_SZ + 4096 * NS

    h1T = mk("h1T", [128, DC, S], BF16, MAIN)
    yT = mk("yT", [128, DC, S], BF16, MAIN + 65536)
    xin = [mk(f"xin{k}", [128, DC, 256], F32, MAIN + 65536 + 16384 * k) for k in range(4)]
    L = MAIN + 131072
    QT = [mk(f"QT{k}", [128, S], BF16, L + 4096 * k) for k in range(2)]
    KT = [mk(f"KT{k}", [128, S], BF16, L + 8192 + 4096 * k) for k in range(2)]
    Vt = [mk(f"Vt{k}", [128, 16, 128], BF16, L + 16384 + 4096 * k) for k in range(2)]
    T0 = L + 24576
    sqA = [mk(f"sqA{k}", [128, 4, 512], BF16, T0 + 4096 * k) for k in range(2)]
    lnvA = mk("lnvA", [128, 512], F32, T0 + 8192)
    rbufA = mk("rbufA", [128, 512], F32, T0 + 10240)
    cst_stage = mk("cststage", [128, 2496], F32, T0 + 12288)
    e_t = [mk(f"e{k}", [128, 512], F32, T0 + 2048 * k) for k in range(2)]
    at_t = [mk(f"at{k}", [128, 512], BF16, T0 + 2048 * k) for k in range(2)]
    spm_t = [mk(f"spm{k}", [128, 512], BF16, T0 + 4096 + 1024 * k) for k in range(2)]
    spb_t = [mk(f"spb{k}", [128, 512], BF16, T0 + 6144 + 1024 * k) for k in range(2)]
    ss_t = [mk(f"ss{k}", [128, 512], BF16, T0 + 8192 + 1024 * k) for k in range(2)]
    a_t = [mk(f"a{k}", [128, 512], BF16, T0 + 10240 + 1024 * k) for k in range(2)]
    z_t = [mk(f"z{k}", [128, 512], F32, T0 + 2048 * k) for k in range(2)]
    ca_t = [mk(f"ca{k}", [128, 512], BF16, T0 + 6144 + 1024 * k) for k in range(2)]
    rden = mk("rden", [128, 512], F32, T0 + 4096)
    BT = [mk(f"BT{k}", [128, 640], F32, T0 + 12288 + 2560 * k) for k in range(2)]
    assert T0 + 12288 + 9984 <= 208768
    sg = mk("sg", [128, S], F32, L)
    sgc = mk("sgc", [128, S], F32, L + 8192)
    mbuf = mk("mbuf", [128, S], F32, L + 16384)
    mgo = [mk(f"mgo{k}", [128, S], BF16, L + 24576 + 4096 * k) for k in range(2)]

    x1 = mk("x1", [128, DC, 1024], F32, MAIN)
    h2 = mk("h2", [128, DC, 1024], BF16, MAIN + 65536)
    act = mk("act", [128, 22, 1024], BF16, MAIN + 98304)
    mg = mk("mg", [128, DC, 1024], BF16, MAIN + 98304)
    L2 = MAIN + 143360
    sqB = [mk(f"sqB{k}", [128, 4, 512], BF16, L2 + 4096 * k) for k in range(2)]
    lnvB = mk("lnvB", [128, 512], F32, L2 + 8192)
    rbufB = mk("rbufB", [128, 512], F32, L2 + 10240)
    R12 = L2 + 12288
    xc = [mk(f"xc{k}", [128, 1024], F32, R12 + 4096 * k) for k in range(2)]
    pTf = mk("pTf", [128, 2, 1024], F32, R12)
    pTb = mk("pTb", [128, 2, 1024], BF16, R12 + 8192)
    rbufF = mk("rbufF", [128, 512], F32, L2 + 32768)
    SX = MAIN + 131072
    stg_loop = [(mk("stg0", [128, 1024], F32, R12 + 8192), ["r4", "r5"], "st0")] + \
               [(mk(f"stgx{i}", [128, 1024], F32, SX + 4096 * i), [f"sx{i}"], f"sx{i}") for i in range(3)]
    stg_tail = stg_loop + [(mk("stg1", [128, 1024], F32, R12), ["r0", "r1"], "st1"),
                           (mk("stg2", [128, 1024], F32, R12 + 4096), ["r2", "r3"], "st2")]
    sl_t = [mk(f"sl{k}", [128, 512], F32, L2 + 24576 + 2048 * k) for k in range(2)]
    tmp_t = [mk(f"tmp{k}", [128, 512], F32, L2 + 28672 + 2048 * k) for k in range(2)]
    upi = [mk(f"upi{k}", [128, 2048], BF16, MAIN + 131072 + 4096 * k) for k in range(2)]

    banks = [nc.alloc_psum_tensor(f"bk{i}", [128, 512], F32) for i in range(8)]

    def ACT(out, in_, func, reads, writes, bias=None, scale=None):
        def fn(e):
            kw = {}
            if bias is not None:
                kw["bias"] = bias
            if scale is not None:
                kw["scale"] = scale
            return e.activation(out=out, in_=in_, func=func, **kw)
        return P.op("act", fn, reads, writes)

    def TT(out, in0, in1, op, reads, writes):
        return P.op("dve", lambda e: e.tensor_tensor(out=out, in0=in0, in1=in1, op=op), reads, writes)

    def STT(out, in0, scalar, in1, op0, op1, reads, writes):
        return P.op("dve", lambda e: e.scalar_tensor_tensor(out=out, in0=in0, scalar=scalar, in1=in1,
                                                            op0=op0, op1=op1), reads, writes)

    def VCOPY(out, in_, reads, writes):
        return P.op("dve", lambda e: e.tensor_copy(out=out, in_=in_), reads, writes)

    def MM(out, pairs, reads, writes, start=True, stop=True, skip=False):
        def fn(e):
            n = len(pairs)
            ins = None
            for i, (l, r) in enumerate(pairs):
                kw = {}
                if skip:
                    kw["skip_group_check"] = True
                ins = e.matmul(out, l, r, start=(start and i == 0), stop=(stop and i == n - 1), **kw)
            return ins
        return P.op("pe", fn, reads, writes)

    def DMA(eng, out, in_, dsem, reads, writes):
        return P.op(eng, lambda e: e.dma_start(out=out, in_=in_), reads, writes, dsem=dsem)

    useq = [0]
    pinned = set()

    def load_unit(u, ncols=2048, pin=False):
        while useq[0] % NS in pinned:
            useq[0] += 1
        k = useq[0] % NS
        useq[0] += 1
        if pin:
            pinned.add(k)
        DMA("pool", slots[k][:, 0:ncols], wall[u][:, 0:ncols], f"w{k}", [], [f"slot{k}"])
        return slots[k], f"slot{k}"

    def unpin(name):
        pinned.discard(int(name[4:]))

    def bkn(i):
        return f"bk{i}"

    def cs(c):
        return slice(c * 128, (c + 1) * 128)

    def ts(t, w=512):
        return slice(t * w, (t + 1) * w)

    DMA("sp", cst_stage[:], cst[:, :], "cst", [], ["cststage"])
    ACT(ones_bf[:], cst_stage[:, 0:128], AF.Copy, ["cststage"], ["ones"])
    ACT(U_bf[:], cst_stage[:, 128:256], AF.Copy, ["cststage"], ["utri"])
    for r in range(4):
        ACT(masks[:, r, :], cst_stage[:, 256 + 512 * r:256 + 512 * (r + 1)], AF.Copy, ["cststage"], ["masks"])
    ACT(gains[:], cst_stage[:, 2304:2368], AF.Copy, ["cststage"], ["gains"])
    ACT(negones[:], cst_stage[:, 2368:2496], AF.Copy, ["cststage"], ["negones"])

    nbank = [0]

    def rms_block(src4, src1, srcnames, gidx, dst, dstnames, sq, lnv, rbuf, tag, qnames=None, cnames=None, W=512):
        bi = nbank[0] % 4
        nbank[0] += 1
        bank = banks[bi]
        for q in range(4):
            s = q % 2
            ACT(sq[s][:, :, 0:W], src4(q), AF.Square, qnames(q) if qnames else srcnames, [f"sq{tag}{s}"])
            MM(bank[:, 0:W], [(ones_bf[:], sq[s][:, i, 0:W]) for i in range(4)], [f"sq{tag}{s}", "ones"], [bkn(bi)],
               start=(q == 0), stop=(q == 3))
        ACT(lnv[:, 0:W], bank[:, 0:W], AF.Ln, [bkn(bi)], [f"lnv{tag}"], bias=EPS, scale=1.0 / D)
        ACT(rbuf[:, 0:W], lnv[:, 0:W], AF.Exp, [f"lnv{tag}"], [f"rbuf{tag}"], scale=-0.5)
        for c in range(DC):
            STT(dst(c), src1(c), gains[:, gidx * 16 + c:gidx * 16 + c + 1], rbuf[:, 0:W], ALU.mult, ALU.mult,
                list(cnames(c) if cnames else srcnames) + [f"rbuf{tag}", "gains"], dstnames(c))

    for sgi in range(8):
        k = sgi % 4
        tg = sgi // 2
        tsl = slice(sgi * 256, (sgi + 1) * 256)
        DMA("sp", xin[k][:], xT_v[:, :, tsl], f"xin{k}", [], [f"xin{k}"])
        rms_block(lambda q, k=k: xin[k][:, 4 * q:4 * q + 4, :], lambda c, k=k: xin[k][:, c, :], [f"xin{k}"], 0,
                  lambda c, tsl=tsl: h1T[:, c, tsl], lambda c, tg=tg: [f"h1T.{tg}"], sqA, lnvA, rbufA, "A", W=256)

    pbank = [0]

    NQUANTA = 96

    def proj_quanta_tgmajor(uq, uk, uv, hp):
        sq_, sqn = load_unit(uq)
        sk_, skn = load_unit(uk)
        sv_, svn = load_unit(uv)
        for tg in range(4):
            bi = 4 + pbank[0] % 2
            pbank[0] += 1
            MM(banks[bi][:], [(sq_[:, cs(c)], h1T[:, c, ts(tg)]) for c in range(DC)], [sqn, f"h1T.{tg}"], [bkn(bi)])
            ACT(QT[hp][:, ts(tg)], banks[bi][:], AF.Copy, [bkn(bi)], [f"QT{hp}.{tg}"], scale=SCALE)
            bi = 4 + pbank[0] % 2
            pbank[0] += 1
            MM(banks[bi][:], [(sk_[:, cs(c)], h1T[:, c, ts(tg)]) for c in range(DC)], [skn, f"h1T.{tg}"], [bkn(bi)])
            VCOPY(KT[hp][:, ts(tg)], banks[bi][:], [bkn(bi)], [f"KT{hp}.{tg}"])
            bi = 4 + pbank[0] % 2
            pbank[0] += 1
            tq = tg
            for i in range(4):
                tt = tq * 4 + i
                MM(banks[bi][:, cs(i)], [(h1T[:, c, cs(tt)], sv_[:, cs(c)]) for c in range(DC)],
                   [svn, f"h1T.{tq}"], [bkn(bi)])
            P.op("act", lambda e, bi=bi, tq=tq: e.activation(
                out=Vt[hp][:, 4 * tq:4 * tq + 4, :], in_=banks[bi][:].rearrange("p (a b) -> p a b", a=4),
                func=AF.Copy), [bkn(bi)], [f"V{hp}.{tq}"])

    def proj_quanta(uq, uk, uv, hp):
        sq_, sqn = load_unit(uq)
        for tg in range(4):
            bi = 4 + pbank[0] % 2
            pbank[0] += 1
            for q8 in range(8):
                MM(banks[bi][:], [(sq_[:, cs(c)], h1T[:, c, ts(tg)]) for c in range(2 * q8, 2 * q8 + 2)],
                   [sqn, f"h1T.{tg}"], [bkn(bi)], start=(q8 == 0), stop=(q8 == 7))
                if q8 == 7:
                    ACT(QT[hp][:, ts(tg)], banks[bi][:], AF.Copy, [bkn(bi)], [f"QT{hp}.{tg}"], scale=SCALE)
                yield
        sk_, skn = load_unit(uk)
        for tg in range(4):
            bi = 4 + pbank[0] % 2
            pbank[0] += 1
            for q8 in range(8):
                MM(banks[bi][:], [(sk_[:, cs(c)], h1T[:, c, ts(tg)]) for c in range(2 * q8, 2 * q8 + 2)],
                   [skn, f"h1T.{tg}"], [bkn(bi)], start=(q8 == 0), stop=(q8 == 7))
                if q8 == 7:
                    VCOPY(KT[hp][:, ts(tg)], banks[bi][:], [bkn(bi)], [f"KT{hp}.{tg}"])
                yield
        sv_, svn = load_unit(uv)
        for tq in range(4):
            bi = 4 + pbank[0] % 2
            pbank[0] += 1
            for i in range(4):
                tt = tq * 4 + i
                for q4 in range(2):
                    MM(banks[bi][:, cs(i)], [(h1T[:, c, cs(tt)], sv_[:, cs(c)]) for c in range(8 * q4, 8 * q4 + 8)],
                       [svn, f"h1T.{tq}"], [bkn(bi)], start=(q4 == 0), stop=(q4 == 1))
                    if i == 3 and q4 == 1:
                        P.op("act", lambda e, bi=bi, tq=tq: e.activation(
                            out=Vt[hp][:, 4 * tq:4 * tq + 4, :], in_=banks[bi][:].rearrange("p (a b) -> p a b", a=4),
                            func=AF.Copy), [bkn(bi)], [f"V{hp}.{tq}"])
                    yield

    class Puller:
        def __init__(self, gen, points):
            self.gen = gen
            self.left = NQUANTA if gen is not None else 0
            self.points = points

        def pull(self):
            if self.gen is None or self.left <= 0:
                self.points -= 1
                return
            k = -(-self.left // max(self.points, 1))
            self.points -= 1
            for _ in range(k):
                try:
                    next(self.gen)
                    self.left -= 1
                except StopIteration:
                    self.left = 0
                    return

        def flush(self):
            if self.gen is None:
                return
            for _ in self.gen:
                pass

    def pull(gen, n):
        if gen is None:
            return
        for _ in range(n):
            try:
                next(gen)
            except StopIteration:
                return

    def sb_attention(h, hp, pl):
        Qh, Kh, Vh = QT[hp], KT[hp], Vt[hp]
        pairs = []
        for g in range(4):
            bl = list(range(4 * g + 3, -1, -1))
            for ig, b in enumerate(bl):
                pairs.append((g, b, ig, len(bl)))
        n = len(pairs)
        for it in range(n + 2):
            if it < n:
                i = it
                g, b, ig, ng = pairs[i]
                k = i % 2
                r = b - 4 * g
                c0 = 128 * r if r > 0 else 0
                if ig == 0:
                    P.op("dve", lambda e, yb=6 + g % 2: e.memset(banks[yb][:], 0.0), [], [bkn(6 + g % 2)])
                MM(banks[k][:, c0:], [(Kh[:, cs(b)], Qh[:, g * 512 + c0:(g + 1) * 512])],
                   [f"KT{hp}.{b // 4}", f"QT{hp}.{g}"], [bkn(k)])
                ACT(e_t[k][:, c0:], banks[k][:, c0:], AF.Exp, [bkn(k)], [f"e{k}"])
                if r >= 0:
                    if c0 > 0:
                        P.op("dve", lambda e, k=k, c0=c0: e.memset(spb_t[k][:, 0:c0], 0.0), [], [f"spb{k}"])
                    ACT(spm_t[k][:, c0:], e_t[k][:, c0:], AF.Ln, [f"e{k}"], [f"spm{k}"], bias=1.0)
                    TT(spb_t[k][:, c0:], spm_t[k][:, c0:], masks[:, r, c0:], ALU.mult, [f"spm{k}", "masks"],
                       [f"spb{k}"])
                else:
                    ACT(spb_t[k][:], e_t[k][:], AF.Ln, [f"e{k}"], [f"spb{k}"], bias=1.0)
            pl.pull()
            if 0 <= it - 1 < n:
                i = it - 1
                g, b, ig, ng = pairs[i]
                k = i % 2
                r = b - 4 * g
                c0 = 128 * r if r > 0 else 0
                prs = [(Kh[:, cs(b)], Qh[:, g * 512 + c0:(g + 1) * 512]), (U_bf[:], spb_t[k][:, c0:])]
                rd = [f"KT{hp}.{b // 4}", f"QT{hp}.{g}", "utri", f"spb{k}"]
                if ig >= 1:
                    prs.append((negones[:], ss_t[k][:, c0:]))
                    rd += ["negones", f"ss{k}"]
                MM(banks[2 + k][:, c0:], prs, rd, [bkn(2 + k)])
                if ig + 1 < ng:
                    if ig == 0:
                        VCOPY(ss_t[1 - k][:], spb_t[k][:], [f"spb{k}"], [f"ss{1 - k}"])
                    else:
                        TT(ss_t[1 - k][:], ss_t[k][:], spb_t[k][:], ALU.add, [f"ss{k}", f"spb{k}"], [f"ss{1 - k}"])
                if r >= 0:
                    ACT(at_t[k][:, c0:], banks[2 + k][:, c0:], AF.Exp, [bkn(2 + k)], [f"e{k}"])
                    TT(a_t[k][:, c0:], at_t[k][:, c0:], masks[:, r, c0:], ALU.mult, [f"e{k}", "masks"], [f"a{k}"])
                else:
                    ACT(a_t[k][:], banks[2 + k][:], AF.Exp, [bkn(2 + k)], [f"a{k}"])
            pl.pull()
            if 0 <= it - 2 < n:
                i = it - 2
                g, b, ig, ng = pairs[i]
                k = i % 2
                r = b - 4 * g
                c0 = 128 * r if r > 0 else 0
                ybi = 6 + g % 2
                MM(banks[ybi][:, c0:], [(Vh[:, b, :], a_t[k][:, c0:])], [f"V{hp}.{b // 4}", f"a{k}"], [bkn(ybi)],
                   start=False, stop=(ig == ng - 1), skip=True)
                if ig == ng - 1:
                    ACT(yT[:, h, ts(g)], banks[ybi][:], AF.Copy, [bkn(ybi)], [f"yT.{g}"])

    def ca_attention(h, hp, btk, pl):
        Qh, Kh, Vh = QT[hp], KT[hp], Vt[hp]
        pairs = []
        for g in range(4):
            first = 4 * g - 1 if g >= 1 else 0
            kts = [first] + [kt for kt in range(max(0, 4 * g - 4), 4 * g + 4) if kt != first]
            for ig, kt in enumerate(kts):
                pairs.append((g, kt, ig, len(kts)))
        n = len(pairs)

        def rng(g, kt):
            lo = max(4 * g, kt)
            hi = min(4 * g + 3, kt + 4)
            c0 = (lo - 4 * g) * 128
            c1 = (hi - 4 * g + 1) * 128
            j0 = (lo - kt) * 128
            return c0, c1, j0

        def finalize(g):
            nb, db = 2 + (g % 2), 6 + (g % 2)
            P.op("dve", lambda e, db=db: e.reciprocal(out=rden[:], in_=banks[db][:]), [bkn(db)], ["spm0", "spm1"])
            TT(yT[:, 8 + h, ts(g)], banks[nb][:], rden[:], ALU.mult, [bkn(nb), "spm0", "spm1"], [f"yT.{g}"])

        pending = []
        for it in range(n + 1):
            if it < n:
                i = it
                g, kt, ig, ng = pairs[i]
                k = i % 2
                c0, c1, j0 = rng(g, kt)
                MM(banks[k][:, c0:c1], [(Kh[:, cs(kt)], Qh[:, g * 512 + c0:g * 512 + c1])],
                   [f"KT{hp}.{kt // 4}", f"QT{hp}.{g}"], [bkn(k)])
                TT(z_t[k][:, c0:c1], banks[k][:, c0:c1], BT[btk][:, j0:j0 + (c1 - c0)], ALU.add,
                   [bkn(k), f"BT{btk}"], [f"e{k}"])
                ACT(ca_t[k][:, c0:c1], z_t[k][:, c0:c1], AF.Exp, [f"e{k}"], [f"spb{k}"])
            pl.pull()
            if 0 <= it - 1 < n:
                i = it - 1
                g, kt, ig, ng = pairs[i]
                k = i % 2
                nb, db = 2 + (g % 2), 6 + (g % 2)
                c0, c1, j0 = rng(g, kt)
                MM(banks[nb][:, c0:c1], [(Vh[:, kt, :], ca_t[k][:, c0:c1])], [f"V{hp}.{kt // 4}", f"spb{k}"],
                   [bkn(nb)], start=(ig == 0), stop=(ig == ng - 1), skip=True)
                MM(banks[db][:, c0:c1], [(ones_bf[:], ca_t[k][:, c0:c1])], ["ones", f"spb{k}"], [bkn(db)],
                   start=(ig == 0), stop=(ig == ng - 1), skip=True)
                if ig == ng - 1:
                    pending.append((g, it + 2))
            while pending and pending[0][1] <= it:
                finalize(pending.pop(0)[0])
        for g, _ in pending:
            finalize(g)

    heads = [("sb", h) for h in range(NH)] + [("ca", h) for h in range(NH)]

    def make_gen(idx):
        kind, h = heads[idx]
        if kind == "sb":
            return proj_quanta(U_QSB + h, U_KSB + h, U_VSB + h, idx % 2)
        DMA("sp", BT[h % 2][:], btd[h], f"bt{h % 2}", [], [f"BT{h % 2}"])
        return proj_quanta(U_QCA + h, U_KCA + h, U_VCA + h, idx % 2)

    proj_quanta_tgmajor(U_QSB, U_KSB, U_VSB, 0)
    SB_POINTS = 2 * (40 + 2)
    CA_POINTS = 28 + 1
    for idx, (kind, h) in enumerate(heads):
        gen = make_gen(idx + 1) if idx + 1 < len(heads) else None
        if kind == "sb":
            pl = Puller(gen, SB_POINTS)
            sb_attention(h, idx % 2, pl)
        else:
            pl = Puller(gen, CA_POINTS)
            ca_attention(h, idx % 2, h % 2, pl)
        pl.flush()

    for j in range(DC):
        ug, ugn = load_unit(U_GSB + j)
        uo, uon = load_unit(U_OUTS + j)
        uc, ucn = load_unit(U_GCA + j)
        for tg in range(4):
            MM(banks[tg][:], [(ug[:, cs(c)], h1T[:, c, ts(tg)]) for c in range(DC)], [ugn, f"h1T.{tg}"], [bkn(tg)])
            ACT(sg[:, ts(tg)], banks[tg][:], AF.Sigmoid, [bkn(tg)], [f"sg.{tg}"])
        for tg in range(4):
            MM(banks[4 + tg][:], [(uo[:, cs(c)], yT[:, c, ts(tg)]) for c in range(8)], [uon, f"yT.{tg}"], [bkn(4 + tg)])
            TT(mbuf[:, ts(tg)], sg[:, ts(tg)], banks[4 + tg][:], ALU.mult, [f"sg.{tg}", bkn(4 + tg)], [f"mbuf.{tg}"])
        for tg in range(4):
            MM(banks[tg][:], [(uc[:, cs(c)], h1T[:, c, ts(tg)]) for c in range(DC)], [ucn, f"h1T.{tg}"], [bkn(tg)])
            ACT(sgc[:, ts(tg)], banks[tg][:], AF.Sigmoid, [bkn(tg)], [f"sgc.{tg}"])
        for tg in range(4):
            MM(banks[4 + tg][:], [(uo[:, cs(8 + c)], yT[:, 8 + c, ts(tg)]) for c in range(8)], [uon, f"yT.{tg}"],
               [bkn(4 + tg)])
            TT(sgc[:, ts(tg)], sgc[:, ts(tg)], banks[4 + tg][:], ALU.mult, [f"sgc.{tg}", bkn(4 + tg)], [f"sgc.{tg}"])
            TT(mgo[j % 2][:, ts(tg)], mbuf[:, ts(tg)], sgc[:, ts(tg)], ALU.add, [f"mbuf.{tg}", f"sgc.{tg}"],
               [f"mgo{j % 2}"])
        DMA("sp", mscr[j], mgo[j % 2][:], f"mst{j % 2}", [f"mgo{j % 2}"], [f"M.{j}"])
    P.barrier()

    def r12(lo, hi):
        return [f"r{i}" for i in range(lo, hi)]

    def x1n(j, t2):
        return f"x1.{j}.{t2}"

    rF = [rbufB, rbufF]
    rFn = ["rbufB", "rbufF"]
    oi = [0]

    def final_out(hb, j, tail=False):
        lst = stg_tail if tail else stg_loop
        buf, nms, ds = lst[oi[0] % len(lst)]
        oi[0] += 1
        for t2 in range(2):
            STT(buf[:, ts(t2)], x1[:, j, ts(t2)], gains[:, 48 + j:49 + j], rF[t2][:], ALU.mult, ALU.mult,
                [x1n(j, t2), rFn[t2], "gains"], nms)
        DMA("sp", outT[j][:, hb * 1024:(hb + 1) * 1024], buf[:], ds, nms, [])

    sli = 0
    sqi = [0]

    def stat_act(j, t2):
        q_ = sqi[0] % 2
        sqi[0] += 1
        ACT(sqB[q_][:, 0, :], x1[:, j, ts(t2)], AF.Square, [x1n(j, t2)], [f"sqB{q_}"])
        return q_

    def stat_pe(j, t2, q_):
        MM(banks[6 + t2][:], [(ones_bf[:], sqB[q_][:, 0, :])], [f"sqB{q_}", "ones"], [bkn(6 + t2)],
           start=(j == 0), stop=(j == DC - 1))

    class Stats:
        def __init__(self):
            self.pend = None
            self.q = None

        def begin(self):
            if self.pend is not None:
                self.q = stat_act(*self.pend)

        def mid(self):
            if self.pend is not None:
                stat_pe(*self.pend, self.q)

        def end(self, j, t2):
            self.pend = (j, t2)

        def finish(self):
            stat_pe(*self.pend, stat_act(*self.pend))
            self.pend = None
            for t2 in range(2):
                ACT(lnvB[:], banks[6 + t2][:], AF.Ln, [bkn(6 + t2)], ["lnvB"], bias=EPS, scale=1.0 / D)
                ACT(rF[t2][:], lnvB[:], AF.Exp, ["lnvB"], [rFn[t2]], scale=-0.5)

    def apply_norm(gidx):
        for t2 in range(2):
            for c in range(DC):
                STT(h2[:, c, ts(t2)], x1[:, c, ts(t2)], gains[:, gidx * 16 + c:gidx * 16 + c + 1], rF[t2][:],
                    ALU.mult, ALU.mult, [x1n(c, t2), rFn[t2], "gains"], [f"h2.{c}.{t2}"])

    def h2n(t2):
        return [f"h2.{c}.{t2}" for c in range(DC)]

    def mm_first(bank_i, wt, wname, t2):
        for c in range(DC):
            MM(banks[bank_i][:], [(wt[:, cs(c)], h2[:, c, ts(t2)])], [wname, f"h2.{c}.{t2}"], [bkn(bank_i)],
               start=(c == 0), stop=(c == DC - 1))

    def t2_first_order(n_items, lead=3):
        steps = [(i, 0) for i in range(lead)] + [(i, 1) for i in range(lead)]
        for i in range(lead, n_items):
            steps += [(i, 0), (i, 1)]
        return steps

    for hb in range(2):
        hsl = slice(hb * 1024, (hb + 1) * 1024)
        DMA("sp", mg[:], mscr_v[:, :, hsl], "mgld", [f"M.{j}" for j in range(DC)], ["actreg"])
        st = Stats()
        DMA("sp", xc[0][:], xT[0][:, hsl], "xc0", [], r12(0, 2))
        for j in range(DC):
            if j + 1 < DC:
                xn = (j + 1) % 2
                DMA("sp", xc[xn][:], xT[j + 1][:, hsl], f"xc{xn}", [], r12(2 * xn, 2 * xn + 2))
            if hb > 0:
                final_out(hb - 1, j)
            um, umn = load_unit(U_MIX + j)
            xk = j % 2
            for t2 in range(2):
                bi = (2 * j + t2) % 6
                st.begin()
                MM(banks[bi][:], [(um[:, cs(c)], mg[:, c, ts(t2)]) for c in range(DC)], [umn, "actreg"], [bkn(bi)])
                st.mid()
                TT(x1[:, j, ts(t2)], xc[xk][:, ts(t2)], banks[bi][:], ALU.add, r12(2 * xk, 2 * xk + 2) + [bkn(bi)],
                   [x1n(j, t2)])
                st.end(j, t2)
        st.finish()
        apply_norm(1)
        for hf in range(2):
            units = {}
            steps = t2_first_order(22) if hf == 0 else [(jj, t2) for jj in range(22) for t2 in range(2)]
            for jj, t2 in steps:
                j = hf * 22 + jj
                if jj not in units:
                    units[jj] = (load_unit(U_FFI + 2 * j), load_unit(U_FFI + 2 * j + 1))
                (ugj, ugn), (uuj, uun) = units[jj]
                bg = (jj % 2) * 4 + t2
                bu = (jj % 2) * 4 + 2 + t2
                if hf == 0 and (jj, t2) == (0, 0):
                    for c in range(DC):
                        MM(banks[bg][:], [(ugj[:, cs(c)], h2[:, c, ts(t2)])], [ugn, f"h2.{c}.{t2}"], [bkn(bg)],
                           start=(c == 0), stop=(c == DC - 1))
                        MM(banks[bu][:], [(uuj[:, cs(c)], h2[:, c, ts(t2)])], [uun, f"h2.{c}.{t2}"], [bkn(bu)],
                           start=(c == 0), stop=(c == DC - 1))
                else:
                    MM(banks[bg][:], [(ugj[:, cs(c)], h2[:, c, ts(t2)]) for c in range(DC)], [ugn] + h2n(t2), [bkn(bg)])
                    MM(banks[bu][:], [(uuj[:, cs(c)], h2[:, c, ts(t2)]) for c in range(DC)], [uun] + h2n(t2), [bkn(bu)])
                s_ = sli % 2
                sli += 1
                ACT(sl_t[s_][:], banks[bg][:], AF.Silu, [bkn(bg)], [f"sl{s_}"])
                wr = ["actreg"] + (["sx0", "sx1", "sx2"] if jj >= 16 else [])
                TT(act[:, jj, ts(t2)], sl_t[s_][:], banks[bu][:], ALU.mult, [f"sl{s_}", bkn(bu)], wr)
            st = Stats() if hf == 1 else None
            for j in range(DC):
                u0, u0n = load_unit(U_FFO + (hf * 16 + j) * 2, 1408)
                u1, u1n = load_unit(U_FFO + (hf * 16 + j) * 2 + 1, 1408)
                for t2 in range(2):
                    bi = (2 * j + t2) % 6
                    prs = [(u0[:, cs(c)], act[:, c, ts(t2)]) for c in range(11)] + \
                          [(u1[:, cs(c)], act[:, 11 + c, ts(t2)]) for c in range(11)]
                    if st:
                        st.begin()
                    MM(banks[bi][:], prs, [u0n, u1n, "actreg", "sx0", "sx1", "sx2"], [bkn(bi)])
                    if st:
                        st.mid()
                    TT(x1[:, j, ts(t2)], x1[:, j, ts(t2)], banks[bi][:], ALU.add, [x1n(j, t2), bkn(bi)], [x1n(j, t2)])
                    if st:
                        st.end(j, t2)
            if st:
                st.finish()
        apply_norm(2)
        DMA("sp", pTf[:], pT_v[:, :, hsl], "pT", [], r12(0, 4))
        for c in range(2):
            ACT(pTb[:, c, :], pTf[:, c, :], AF.Copy, r12(0, 4), r12(4, 6))
        st = Stats()
        units = {}
        pis = {}
        for j, t2 in t2_first_order(DC):
            if j // 8 not in pis:
                if j // 8 == 1:
                    unpin(pis[0][1])
                pis[j // 8] = load_unit(U_PI + j // 8, pin=True)
            if j not in units:
                units[j] = load_unit(U_PG + j)
            ug, ugn = units[j]
            upk, upkn = pis[j // 8]
            bg = (j % 2) * 2 + t2
            bp = 4 + t2
            st.begin()
            if (j, t2) == (0, 0):
                mm_first(bg, ug, ugn, t2)
            else:
                MM(banks[bg][:], [(ug[:, cs(c)], h2[:, c, ts(t2)]) for c in range(DC)], [ugn] + h2n(t2), [bkn(bg)])
            MM(banks[bp][:], [(upk[:, cs((j % 8) * 2 + c)], pTb[:, c, ts(t2)]) for c in range(2)],
               [upkn] + r12(4, 6), [bkn(bp)])
            st.mid()
            s_ = sli % 2
            sli += 1
            ACT(sl_t[s_][:], banks[bg][:], AF.Sigmoid, [bkn(bg)], [f"sl{s_}"])
            TT(tmp_t[s_][:], sl_t[s_][:], banks[bp][:], ALU.mult, [f"sl{s_}", bkn(bp)], [f"tmp{s_}"])
            TT(x1[:, j, ts(t2)], x1[:, j, ts(t2)], tmp_t[s_][:], ALU.add, [x1n(j, t2), f"tmp{s_}"], [x1n(j, t2)])
            st.end(j, t2)
        st.finish()
        unpin(pis[1][1])
    for j in range(DC):
        final_out(1, j, tail=True)

    P.op("sp", None, extra=[o for o in P.last_dma.values()])

    P.finalize()
    engsem = {e: nc.alloc_semaphore(name=f"sem_{e}") for e in ("pe", "act", "dve", "pool")}
    dmasem = {nm: nc.alloc_semaphore(name=f"dsem_{nm}") for nm in P.dma_cnt}
    with nc.Block() as block:
        P.emit(nc, block, engsem, dmasem)
    return nc


def _unit(wsub):
    n = wsub.shape[0] // 128
    u = np.zeros((128, 2048), np.float32)
    u[:, :n * 128] = wsub.reshape(n, 128, 128).transpose(1, 0, 2).reshape(128, n * 128)
    return u


def _build_wall(w_in, w_sb_out, w_ca_out, w_mix_out, w_ffn_in, w_ffn_out, w_ple_in, w_ple_gate):
    wall = np.zeros((NU, 128, 2048), np.float32)
    for j in range(80):
        wall[j] = _unit(w_in[:, j * 128:(j + 1) * 128])
    outs = np.concatenate([w_sb_out, w_ca_out], axis=0)
    for j in range(16):
        wall[U_OUTS + j] = _unit(outs[:, j * 128:(j + 1) * 128])
        wall[U_MIX + j] = _unit(w_mix_out[:, j * 128:(j + 1) * 128])
        wall[U_PG + j] = _unit(w_ple_gate[:, j * 128:(j + 1) * 128])
    for j in range(FC):
        wall[U_FFI + 2 * j] = _unit(w_ffn_in[:, j * 128:(j + 1) * 128])
        wall[U_FFI + 2 * j + 1] = _unit(w_ffn_in[:, DFF + j * 128:DFF + (j + 1) * 128])
    for hf in range(2):
        for j in range(16):
            for part in range(2):
                r0 = (hf * 22 + part * 11) * 128
                wall[U_FFO + (hf * 16 + j) * 2 + part] = _unit(w_ffn_out[r0:r0 + 11 * 128, j * 128:(j + 1) * 128])
    for half in range(2):
        u = np.zeros((128, 2048), np.float32)
        for jj in range(8):
            j = half * 8 + jj
            blk = w_ple_in[:, j * 128:(j + 1) * 128]
            u[:, jj * 256:(jj + 1) * 256] = blk.reshape(2, 128, 128).transpose(1, 0, 2).reshape(128, 256)
        wall[U_PI + half] = u
    return wall


def _build_consts(g_mix, g_ffn, g_ple, g_final, rel_bias):
    cst = np.zeros((128, 2496), np.float32)
    cst[:, 2368:2496] = -1.0
    cst[:, 0:128] = 1.0
    jj = np.arange(128)[:, None]
    ss = np.arange(128)[None, :]
    cst[:, 128:256] = -(jj >= ss).astype(np.float32)
    p = np.arange(128)[:, None]
    t = np.arange(512)[None, :]
    for r in range(4):
        cst[:, 256 + 512 * r:256 + 512 * (r + 1)] = (t > 128 * r + p).astype(np.float32)
    for i, g in enumerate((g_mix, g_ffn, g_ple, g_final)):
        cst[:, 2304 + 16 * i:2304 + 16 * (i + 1)] = np.asarray(g, np.float32).reshape(16, 128).T
    pj = np.arange(128)[:, None]
    jq = np.arange(640)[None, :]
    idx = np.clip(pj - jq, -128, 63) + 128
    dchunk = (jq // 64) - (pj // 64)
    valid = (dchunk >= 0) & (dchunk <= 8)
    bt = np.where(valid[None], rel_bias[:, idx], np.float32(NEGM)).astype(np.float32)
    return cst, bt


_NC_CACHE = {}


def kernel(x, p, w_in, w_sb_out, w_ca_out, w_mix_out, rel_bias, g_mix, g_ffn, g_ple, g_final,
           w_ffn_in, w_ffn_out, w_ple_in, w_ple_gate):
    f = lambda a: np.asarray(a, dtype=np.float32)
    x = f(x)
    p = f(p)
    wall = _build_wall(f(w_in)[0], f(w_sb_out)[0], f(w_ca_out)[0], f(w_mix_out)[0], f(w_ffn_in)[0],
                       f(w_ffn_out)[0], f(w_ple_in)[0], f(w_ple_gate)[0])
    cst, bt = _build_consts(f(g_mix)[0], f(g_ffn)[0], f(g_ple)[0], f(g_final), f(rel_bias)[0])
    if "nc" not in _NC_CACHE:
        _NC_CACHE["nc"] = build_program()
    nc = _NC_CACHE["nc"]
    in_maps = []
    for b in range(NCORES):
        xTb = np.ascontiguousarray(x[b].T).reshape(DC, 128, S)
        pTb = np.ascontiguousarray(p[0, b].T).reshape(2, 128, S)
        in_maps.append({"xT": xTb, "pT": pTb, "wall": wall, "cst": cst, "bt": bt})
    res = run_bass_kernel_spmd(nc, in_maps, core_ids=list(range(NCORES)))
    out = np.empty((NCORES, S, D), np.float32)
    for b in range(NCORES):
        out[b] = res.results[b]["outT"].reshape(D, S).T
    return out
```

```python
import numpy as np
import concourse.bass as bass
import concourse.mybir as mybir
from concourse.bass_utils import run_bass_kernel_spmd

F32 = mybir.dt.float32
BF16 = mybir.dt.bfloat16
AF = mybir.ActivationFunctionType
ALU = mybir.AluOpType

D = 2048
S = 2048
DC = 16
NH = 8
HD = 128
DFF = 5632
FC = 44
EPS = 1e-6
SCALE = HD ** -0.5
NCORES = 8
NS = 6
NEGM = -30000.0
STRICT_SAME_ENGINE = True

U_QSB, U_KSB, U_VSB, U_QCA, U_KCA, U_VCA, U_GSB, U_GCA = 0, 8, 16, 24, 32, 40, 48, 64
U_OUTS = 80
U_MIX = 96
U_FFI = 112
U_FFO = 200
U_PG = 264
U_PI = 280
NU = 282


class Op:
    __slots__ = ("eng", "fn", "raw", "oth", "sig", "cnt", "dsem", "dval", "idx")

    def __init__(self, eng, fn):
        self.eng = eng
        self.fn = fn
        self.raw = []
        self.oth = []
        self.sig = False
        self.cnt = 0
        self.dsem = None
        self.dval = 0


class Prog:
    ENGS = ("pe", "act", "dve", "pool", "sp")

    def __init__(self):
        self.streams = {e: [] for e in self.ENGS}
        self.lastw = {}
        self.readers = {}
        self.dma_cnt = {}
        self.last_dma = {}

    def op(self, eng, fn, reads=(), writes=(), dsem=None, extra=()):
        o = Op(eng, fn)
        raw, oth = set(), set()
        for b in reads:
            w = self.lastw.get(b)
            if w is not None:
                raw.add(w)
        for b in writes:
            w = self.lastw.get(b)
            if w is not None:
                oth.add(w)
            for r in self.readers.get(b, {}).values():
                oth.add(r)
        for x in extra:
            raw.add(x)
        raw.discard(o)
        oth -= raw
        o.raw = list(raw)
        o.oth = list(oth)
        key = dsem if dsem is not None else eng
        for b in reads:
            self.readers.setdefault(b, {})[key] = o
        for b in writes:
            self.lastw[b] = o
            self.readers[b] = {}
        if dsem is not None:
            c = self.dma_cnt.get(dsem, 0) + 1
            self.dma_cnt[dsem] = c
            o.dsem = dsem
            o.dval = 16 * c
            self.last_dma[dsem] = o
        o.idx = len(self.streams[eng])
        self.streams[eng].append(o)
        return o

    def barrier(self, engs=("pe", "act", "dve", "sp")):
        lasts = []
        for e in ("pe", "act", "dve"):
            if self.streams[e]:
                lasts.append(self.streams[e][-1])
        for nm, o in self.last_dma.items():
            if not nm.startswith("w"):
                lasts.append(o)
        for e in engs:
            self.op(e, None, extra=lasts)

    def _needed(self, o):
        res = []
        for d in o.raw:
            if d.dsem is not None:
                res.append(d)
            elif d.eng == o.eng:
                if o.eng != "pe" and d.fn is not None:
                    res.append(d)
            else:
                res.append(d)
        for d in o.oth:
            if d.dsem is not None:
                res.append(d)
            elif d.eng != o.eng:
                res.append(d)
            elif STRICT_SAME_ENGINE and o.eng != "pe" and d.fn is not None:
                res.append(d)
        return res

    def finalize(self):
        for e in self.ENGS:
            for o in self.streams[e]:
                for d in self._needed(o):
                    if d.dsem is None:
                        d.sig = True
        for e in self.ENGS:
            cnt = 0
            for o in self.streams[e]:
                if o.dsem is None and o.sig and o.fn is not None:
                    cnt += 1
                o.cnt = cnt

    def emit(self, nc, block, engsem, dmasem):
        prog = self

        def run(ename, e):
            seen = {}
            for o in prog.streams[ename]:
                for d in prog._needed(o):
                    if d.dsem is not None:
                        sem, val = dmasem[d.dsem], d.dval
                    else:
                        sem, val = engsem[d.eng], d.cnt
                    if val <= 0:
                        continue
                    k = sem.num
                    if seen.get(k, 0) < val:
                        e.wait_ge(sem, val)
                        seen[k] = val
                if o.fn is not None:
                    ins = o.fn(e)
                    if o.dsem is not None:
                        ins.then_inc(dmasem[o.dsem], 16)
                    elif o.sig:
                        ins.then_inc(engsem[ename], 1)

        @block.tensor
        def _(e):
            run("pe", e)

        @block.scalar
        def _(e):
            run("act", e)

        @block.vector
        def _(e):
            run("dve", e)

        @block.gpsimd
        def _(e):
            run("pool", e)

        @block.sync
        def _(e):
            run("sp", e)


def build_program():
    nc = bass.Bass("TRN2", target_bir_lowering=False)
    P = Prog()

    xT = nc.dram_tensor("xT", [DC, 128, S], F32, kind="ExternalInput").ap()
    pT = nc.dram_tensor("pT", [2, 128, S], F32, kind="ExternalInput").ap()
    wall = nc.dram_tensor("wall", [NU, 128, 2048], F32, kind="ExternalInput").ap()
    cst = nc.dram_tensor("cst", [128, 2496], F32, kind="ExternalInput").ap()
    btd = nc.dram_tensor("bt", [NH, 128, 640], F32, kind="ExternalInput").ap()
    outT = nc.dram_tensor("outT", [DC, 128, S], F32, kind="ExternalOutput").ap()
    mscr = nc.dram_tensor("mscr", [DC, 128, S], BF16).ap()

    xT_v = xT.rearrange("c p t -> p c t")
    pT_v = pT.rearrange("c p t -> p c t")
    mscr_v = mscr.rearrange("c p t -> p c t")

    M0 = 20608
    cnt = [0]

    def mk(name, shape, dtype, off):
        cnt[0] += 1
        return nc.alloc_sbuf_tensor_at(f"{name}_{cnt[0]}", shape, dtype, offset=M0 + off)

    ones_bf = mk("ones", [128, 128], BF16, 0)
    U_bf = mk("utri", [128, 128], BF16, 256)
    masks = mk("masks", [128, 4, 512], BF16, 512)
    gains = mk("gains", [128, 64], F32, 4608)
    negones = mk("negones", [128, 128], BF16, 4864)
    CONST_SZ = 5120
    slots = [mk(f"slot{k}", [128, 2048], BF16, CONST_SZ + 4096 * k) for k in range(NS)]
    MAIN = CONST_SZ + 4096 * NS

    h1T = mk("h1T", [128, DC, S], BF16, MAIN)
    yT = mk("yT", [128, DC, S], BF16, MAIN + 65536)
    xin = [mk(f"xin{k}", [128, DC, 512], F32, MAIN + 65536 + 32768 * k) for k in range(2)]
    L = MAIN + 131072
    QT = [mk(f"QT{k}", [128, S], BF16, L + 4096 * k) for k in range(2)]
    KT = [mk(f"KT{k}", [128, S], BF16, L + 8192 + 4096 * k) for k in range(2)]
    Vt = [mk(f"Vt{k}", [128, 16, 128], BF16, L + 16384 + 4096 * k) for k in range(2)]
    T0 = L + 24576
    sqA = [mk(f"sqA{k}", [128, 4, 512], BF16, T0 + 4096 * k) for k in range(2)]
    lnvA = mk("lnvA", [128, 512], F32, T0 + 8192)
    rbufA = mk("rbufA", [128, 512], F32, T0 + 10240)
    cst_stage = mk("cststage", [128, 2496], F32, T0 + 12288)
    e_t = [mk(f"e{k}", [128, 512], F32, T0 + 2048 * k) for k in range(2)]
    at_t = [mk(f"at{k}", [128, 512], BF16, T0 + 2048 * k) for k in range(2)]
    spm_t = [mk(f"spm{k}", [128, 512], BF16, T0 + 4096 + 1024 * k) for k in range(2)]
    spb_t = [mk(f"spb{k}", [128, 512], BF16, T0 + 6144 + 1024 * k) for k in range(2)]
    ss_t = [mk(f"ss{k}", [128, 512], BF16, T0 + 8192 + 1024 * k) for k in range(2)]
    a_t = [mk(f"a{k}", [128, 512], BF16, T0 + 10240 + 1024 * k) for k in range(2)]
    z_t = [mk(f"z{k}", [128, 512], F32, T0 + 2048 * k) for k in range(2)]
    ca_t = [mk(f"ca{k}", [128, 512], BF16, T0 + 6144 + 1024 * k) for k in range(2)]
    rden = mk("rden", [128, 512], F32, T0 + 4096)
    BT = [mk(f"BT{k}", [128, 640], F32, T0 + 12288 + 2560 * k) for k in range(2)]
    assert T0 + 12288 + 9984 <= 208768
    sg = mk("sg", [128, S], F32, L)
    sgc = mk("sgc", [128, S], F32, L + 8192)
    mbuf = mk("mbuf", [128, S], F32, L + 16384)
    mgo = [mk(f"mgo{k}", [128, S], BF16, L + 24576 + 4096 * k) for k in range(2)]

    x1 = mk("x1", [128, DC, 1024], F32, MAIN)
    h2 = mk("h2", [128, DC, 1024], BF16, MAIN + 65536)
    act = mk("act", [128, 22, 1024], BF16, MAIN + 98304)
    mg = mk("mg", [128, DC, 1024], BF16, MAIN + 98304)
    L2 = MAIN + 143360
    sqB = [mk(f"sqB{k}", [128, 4, 512], BF16, L2 + 4096 * k) for k in range(2)]
    lnvB = mk("lnvB", [128, 512], F32, L2 + 8192)
    rbufB = mk("rbufB", [128, 512], F32, L2 + 10240)
    R12 = L2 + 12288
    xc = [mk(f"xc{k}", [128, 1024], F32, R12 + 4096 * k) for k in range(2)]
    pTf = mk("pTf", [128, 2, 1024], F32, R12)
    pTb = mk("pTb", [128, 2, 1024], BF16, R12 + 8192)
    rbufF = mk("rbufF", [128, 512], F32, L2 + 32768)
    SX = MAIN + 131072
    stg_loop = [(mk("stg0", [128, 1024], F32, R12 + 8192), ["r4", "r5"], "st0")] + \
               [(mk(f"stgx{i}", [128, 1024], F32, SX + 4096 * i), [f"sx{i}"], f"sx{i}") for i in range(3)]
    stg_tail = stg_loop + [(mk("stg1", [128, 1024], F32, R12), ["r0", "r1"], "st1"),
                           (mk("stg2", [128, 1024], F32, R12 + 4096), ["r2", "r3"], "st2")]
    sl_t = [mk(f"sl{k}", [128, 512], F32, L2 + 24576 + 2048 * k) for k in range(2)]
    tmp_t = [mk(f"tmp{k}", [128, 512], F32, L2 + 28672 + 2048 * k) for k in range(2)]
    upi = [mk(f"upi{k}", [128, 2048], BF16, MAIN + 131072 + 4096 * k) for k in range(2)]

    banks = [nc.alloc_psum_tensor(f"bk{i}", [128, 512], F32) for i in range(8)]

    def ACT(out, in_, func, reads, writes, bias=None, scale=None):
        def fn(e):
            kw = {}
            if bias is not None:
                kw["bias"] = bias
            if scale is not None:
                kw["scale"] = scale
            return e.activation(out=out, in_=in_, func=func, **kw)
        return P.op("act", fn, reads, writes)

    def TT(out, in0, in1, op, reads, writes):
        return P.op("dve", lambda e: e.tensor_tensor(out=out, in0=in0, in1=in1, op=op), reads, writes)

    def STT(out, in0, scalar, in1, op0, op1, reads, writes):
        return P.op("dve", lambda e: e.scalar_tensor_tensor(out=out, in0=in0, scalar=scalar, in1=in1,
                                                            op0=op0, op1=op1), reads, writes)

    def VCOPY(out, in_, reads, writes):
        return P.op("dve", lambda e: e.tensor_copy(out=out, in_=in_), reads, writes)

    def MM(out, pairs, reads, writes, start=True, stop=True, skip=False):
        def fn(e):
            n = len(pairs)
            ins = None
            for i, (l, r) in enumerate(pairs):
                kw = {}
                if skip:
                    kw["skip_group_check"] = True
                ins = e.matmul(out, l, r, start=(start and i == 0), stop=(stop and i == n - 1), **kw)
            return ins
        return P.op("pe", fn, reads, writes)

    def DMA(eng, out, in_, dsem, reads, writes):
        return P.op(eng, lambda e: e.dma_start(out=out, in_=in_), reads, writes, dsem=dsem)

    useq = [0]
    pinned = set()

    def load_unit(u, ncols=2048, pin=False):
        while useq[0] % NS in pinned:
            useq[0] += 1
        k = useq[0] % NS
        useq[0] += 1
        if pin:
            pinned.add(k)
        DMA("pool", slots[k][:, 0:ncols], wall[u][:, 0:ncols], f"w{k}", [], [f"slot{k}"])
        return slots[k], f"slot{k}"

    def unpin(name):
        pinned.discard(int(name[4:]))

    def bkn(i):
        return f"bk{i}"

    def cs(c):
        return slice(c * 128, (c + 1) * 128)

    def ts(t, w=512):
        return slice(t * w, (t + 1) * w)

    DMA("sp", cst_stage[:], cst[:, :], "cst", [], ["cststage"])
    ACT(ones_bf[:], cst_stage[:, 0:128], AF.Copy, ["cststage"], ["ones"])
    ACT(U_bf[:], cst_stage[:, 128:256], AF.Copy, ["cststage"], ["utri"])
    for r in range(4):
        ACT(masks[:, r, :], cst_stage[:, 256 + 512 * r:256 + 512 * (r + 1)], AF.Copy, ["cststage"], ["masks"])
    ACT(gains[:], cst_stage[:, 2304:2368], AF.Copy, ["cststage"], ["gains"])
    ACT(negones[:], cst_stage[:, 2368:2496], AF.Copy, ["cststage"], ["negones"])

    nbank = [0]

    def rms_block(src4, src1, srcnames, gidx, dst, dstnames, sq, lnv, rbuf, tag, qnames=None, cnames=None):
        bi = nbank[0] % 4
        nbank[0] += 1
        bank = banks[bi]
        for q in range(4):
            s = q % 2
            ACT(sq[s][:], src4(q), AF.Square, qnames(q) if qnames else srcnames, [f"sq{tag}{s}"])
            MM(bank[:], [(ones_bf[:], sq[s][:, i, :]) for i in range(4)], [f"sq{tag}{s}", "ones"], [bkn(bi)],
               start=(q == 0), stop=(q == 3))
        ACT(lnv[:], bank[:], AF.Ln, [bkn(bi)], [f"lnv{tag}"], bias=EPS, scale=1.0 / D)
        ACT(rbuf[:], lnv[:], AF.Exp, [f"lnv{tag}"], [f"rbuf{tag}"], scale=-0.5)
        for c in range(DC):
            STT(dst(c), src1(c), gains[:, gidx * 16 + c:gidx * 16 + c + 1], rbuf[:], ALU.mult, ALU.mult,
                list(cnames(c) if cnames else srcnames) + [f"rbuf{tag}", "gains"], dstnames(c))

    for tg in range(4):
        k = tg % 2
        DMA("sp", xin[k][:], xT_v[:, :, ts(tg)], f"xin{k}", [], [f"xin{k}"])
        rms_block(lambda q, k=k: xin[k][:, 4 * q:4 * q + 4, :], lambda c, k=k: xin[k][:, c, :], [f"xin{k}"], 0,
                  lambda c, tg=tg: h1T[:, c, ts(tg)], lambda c, tg=tg: [f"h1T.{tg}"], sqA, lnvA, rbufA, "A")

    pbank = [0]

    NQUANTA = 96

    def proj_quanta_tgmajor(uq, uk, uv, hp):
        sq_, sqn = load_unit(uq)
        sk_, skn = load_unit(uk)
        sv_, svn = load_unit(uv)
        for tg in range(4):
            bi = 4 + pbank[0] % 2
            pbank[0] += 1
            MM(banks[bi][:], [(sq_[:, cs(c)], h1T[:, c, ts(tg)]) for c in range(DC)], [sqn, f"h1T.{tg}"], [bkn(bi)])
            ACT(QT[hp][:, ts(tg)], banks[bi][:], AF.Copy, [bkn(bi)], [f"QT{hp}.{tg}"], scale=SCALE)
            bi = 4 + pbank[0] % 2
            pbank[0] += 1
            MM(banks[bi][:], [(sk_[:, cs(c)], h1T[:, c, ts(tg)]) for c in range(DC)], [skn, f"h1T.{tg}"], [bkn(bi)])
            VCOPY(KT[hp][:, ts(tg)], banks[bi][:], [bkn(bi)], [f"KT{hp}.{tg}"])
            bi = 4 + pbank[0] % 2
            pbank[0] += 1
            tq = tg
            for i in range(4):
                tt = tq * 4 + i
                MM(banks[bi][:, cs(i)], [(h1T[:, c, cs(tt)], sv_[:, cs(c)]) for c in range(DC)],
                   [svn, f"h1T.{tq}"], [bkn(bi)])
            P.op("act", lambda e, bi=bi, tq=tq: e.activation(
                out=Vt[hp][:, 4 * tq:4 * tq + 4, :], in_=banks[bi][:].rearrange("p (a b) -> p a b", a=4),
                func=AF.Copy), [bkn(bi)], [f"V{hp}.{tq}"])

    def proj_quanta(uq, uk, uv, hp):
        sq_, sqn = load_unit(uq)
        for tg in range(4):
            bi = 4 + pbank[0] % 2
            pbank[0] += 1
            for q8 in range(8):
                MM(banks[bi][:], [(sq_[:, cs(c)], h1T[:, c, ts(tg)]) for c in range(2 * q8, 2 * q8 + 2)],
                   [sqn, f"h1T.{tg}"], [bkn(bi)], start=(q8 == 0), stop=(q8 == 7))
                if q8 == 7:
                    ACT(QT[hp][:, ts(tg)], banks[bi][:], AF.Copy, [bkn(bi)], [f"QT{hp}.{tg}"], scale=SCALE)
                yield
        sk_, skn = load_unit(uk)
        for tg in range(4):
            bi = 4 + pbank[0] % 2
            pbank[0] += 1
            for q8 in range(8):
                MM(banks[bi][:], [(sk_[:, cs(c)], h1T[:, c, ts(tg)]) for c in range(2 * q8, 2 * q8 + 2)],
                   [skn, f"h1T.{tg}"], [bkn(bi)], start=(q8 == 0), stop=(q8 == 7))
                if q8 == 7:
                    VCOPY(KT[hp][:, ts(tg)], banks[bi][:], [bkn(bi)], [f"KT{hp}.{tg}"])
                yield
        sv_, svn = load_unit(uv)
        for tq in range(4):
            bi = 4 + pbank[0] % 2
            pbank[0] += 1
            for i in range(4):
                tt = tq * 4 + i
                for q4 in range(2):
                    MM(banks[bi][:, cs(i)], [(h1T[:, c, cs(tt)], sv_[:, cs(c)]) for c in range(8 * q4, 8 * q4 + 8)],
                       [svn, f"h1T.{tq}"], [bkn(bi)], start=(q4 == 0), stop=(q4 == 1))
                    if i == 3 and q4 == 1:
                        P.op("act", lambda e, bi=bi, tq=tq: e.activation(
                            out=Vt[hp][:, 4 * tq:4 * tq + 4, :], in_=banks[bi][:].rearrange("p (a b) -> p a b", a=4),
                            func=AF.Copy), [bkn(bi)], [f"V{hp}.{tq}"])
                    yield

    class Puller:
        def __init__(self, gen, points):
            self.gen = gen
            self.left = NQUANTA if gen is not None else 0
            self.points = points

        def pull(self):
            if self.gen is None or self.left <= 0:
                self.points -= 1
                return
            k = -(-self.left // max(self.points, 1))
            self.points -= 1
            for _ in range(k):
                try:
                    next(self.gen)
                    self.left -= 1
                except StopIteration:
                    self.left = 0
                    return

        def flush(self):
            if self.gen is None:
                return
            for _ in self.gen:
                pass

    def pull(gen, n):
        if gen is None:
            return
        for _ in range(n):
            try:
                next(gen)
            except StopIteration:
                return

    def sb_attention(h, hp, pl):
        Qh, Kh, Vh = QT[hp], KT[hp], Vt[hp]
        pairs = []
        for g in range(4):
            bl = list(range(4 * g + 3, -1, -1))
            for ig, b in enumerate(bl):
                pairs.append((g, b, ig, len(bl)))
        n = len(pairs)
        for it in range(n + 2):
            if it < n:
                i = it
                g, b, ig, ng = pairs[i]
                k = i % 2
                r = b - 4 * g
                c0 = 128 * r if r > 0 else 0
                if ig == 0:
                    P.op("dve", lambda e, yb=6 + g % 2: e.memset(banks[yb][:], 0.0), [], [bkn(6 + g % 2)])
                kb = i % 4
                MM(banks[kb][:, c0:], [(Kh[:, cs(b)], Qh[:, g * 512 + c0:(g + 1) * 512])],
                   [f"KT{hp}.{b // 4}", f"QT{hp}.{g}"], [bkn(kb)], start=True, stop=False, skip=True)
                ACT(e_t[k][:, c0:], banks[kb][:, c0:], AF.Exp, [bkn(kb)], [f"e{k}"])
                if r >= 0:
                    if c0 > 0:
                        P.op("dve", lambda e, k=k, c0=c0: e.memset(spb_t[k][:, 0:c0], 0.0), [], [f"spb{k}"])
                    ACT(spm_t[k][:, c0:], e_t[k][:, c0:], AF.Ln, [f"e{k}"], [f"spm{k}"], bias=1.0)
                    TT(spb_t[k][:, c0:], spm_t[k][:, c0:], masks[:, r, c0:], ALU.mult, [f"spm{k}", "masks"],
                       [f"spb{k}"])
                else:
                    ACT(spb_t[k][:], e_t[k][:], AF.Ln, [f"e{k}"], [f"spb{k}"], bias=1.0)
            pl.pull()
            if 0 <= it - 1 < n:
                i = it - 1
                g, b, ig, ng = pairs[i]
                k = i % 2
                r = b - 4 * g
                c0 = 128 * r if r > 0 else 0
                kb = i % 4
                prs = [(U_bf[:], spb_t[k][:, c0:])]
                rd = ["utri", f"spb{k}"]
                if ig >= 1:
                    prs.append((negones[:], ss_t[k][:, c0:]))
                    rd += ["negones", f"ss{k}"]
                MM(banks[kb][:, c0:], prs, rd, [bkn(kb)], start=False, stop=True, skip=True)
                if ig + 1 < ng:
                    if ig == 0:
                        VCOPY(ss_t[1 - k][:], spb_t[k][:], [f"spb{k}"], [f"ss{1 - k}"])
                    else:
                        TT(ss_t[1 - k][:], ss_t[k][:], spb_t[k][:], ALU.add, [f"ss{k}", f"spb{k}"], [f"ss{1 - k}"])
                if r >= 0:
                    ACT(at_t[k][:, c0:], banks[kb][:, c0:], AF.Exp, [bkn(kb)], [f"e{k}"])
                    TT(a_t[k][:, c0:], at_t[k][:, c0:], masks[:, r, c0:], ALU.mult, [f"e{k}", "masks"], [f"a{k}"])
                else:
                    ACT(a_t[k][:], banks[kb][:], AF.Exp, [bkn(kb)], [f"a{k}"])
            pl.pull()
            if 0 <= it - 2 < n:
                i = it - 2
                g, b, ig, ng = pairs[i]
                k = i % 2
                r = b - 4 * g
                c0 = 128 * r if r > 0 else 0
                ybi = 6 + g % 2
                MM(banks[ybi][:, c0:], [(Vh[:, b, :], a_t[k][:, c0:])], [f"V{hp}.{b // 4}", f"a{k}"], [bkn(ybi)],
                   start=False, stop=(ig == ng - 1), skip=True)
                if ig == ng - 1:
                    ACT(yT[:, h, ts(g)], banks[ybi][:], AF.Copy, [bkn(ybi)], [f"yT.{g}"])

    def ca_attention(h, hp, btk, pl):
        Qh, Kh, Vh = QT[hp], KT[hp], Vt[hp]
        pairs = []
        for g in range(4):
            first = 4 * g - 1 if g >= 1 else 0
            kts = [first] + [kt for kt in range(max(0, 4 * g - 4), 4 * g + 4) if kt != first]
            for ig, kt in enumerate(kts):
                pairs.append((g, kt, ig, len(kts)))
        n = len(pairs)

        def rng(g, kt):
            lo = max(4 * g, kt)
            hi = min(4 * g + 3, kt + 4)
            c0 = (lo - 4 * g) * 128
            c1 = (hi - 4 * g + 1) * 128
            j0 = (lo - kt) * 128
            return c0, c1, j0

        def finalize(g):
            nb, db = 2 + (g % 2), 6 + (g % 2)
            P.op("dve", lambda e, db=db: e.reciprocal(out=rden[:], in_=banks[db][:]), [bkn(db)], ["spm0", "spm1"])
            TT(yT[:, 8 + h, ts(g)], banks[nb][:], rden[:], ALU.mult, [bkn(nb), "spm0", "spm1"], [f"yT.{g}"])

        pending = []
        for it in range(n + 1):
            if it < n:
                i = it
                g, kt, ig, ng = pairs[i]
                k = i % 2
                c0, c1, j0 = rng(g, kt)
                MM(banks[k][:, c0:c1], [(Kh[:, cs(kt)], Qh[:, g * 512 + c0:g * 512 + c1])],
                   [f"KT{hp}.{kt // 4}", f"QT{hp}.{g}"], [bkn(k)])
                TT(z_t[k][:, c0:c1], banks[k][:, c0:c1], BT[btk][:, j0:j0 + (c1 - c0)], ALU.add,
                   [bkn(k), f"BT{btk}"], [f"e{k}"])
                ACT(ca_t[k][:, c0:c1], z_t[k][:, c0:c1], AF.Exp, [f"e{k}"], [f"spb{k}"])
            pl.pull()
            if 0 <= it - 1 < n:
                i = it - 1
                g, kt, ig, ng = pairs[i]
                k = i % 2
                nb, db = 2 + (g % 2), 6 + (g % 2)
                c0, c1, j0 = rng(g, kt)
                MM(banks[nb][:, c0:c1], [(Vh[:, kt, :], ca_t[k][:, c0:c1])], [f"V{hp}.{kt // 4}", f"spb{k}"],
                   [bkn(nb)], start=(ig == 0), stop=(ig == ng - 1), skip=True)
                MM(banks[db][:, c0:c1], [(ones_bf[:], ca_t[k][:, c0:c1])], ["ones", f"spb{k}"], [bkn(db)],
                   start=(ig == 0), stop=(ig == ng - 1), skip=True)
                if ig == ng - 1:
                    pending.append((g, it + 2))
            while pending and pending[0][1] <= it:
                finalize(pending.pop(0)[0])
        for g, _ in pending:
            finalize(g)

    heads = [("sb", h) for h in range(NH)] + [("ca", h) for h in range(NH)]

    def make_gen(idx):
        kind, h = heads[idx]
        if kind == "sb":
            return proj_quanta(U_QSB + h, U_KSB + h, U_VSB + h, idx % 2)
        DMA("sp", BT[h % 2][:], btd[h], f"bt{h % 2}", [], [f"BT{h % 2}"])
        return proj_quanta(U_QCA + h, U_KCA + h, U_VCA + h, idx % 2)

    proj_quanta_tgmajor(U_QSB, U_KSB, U_VSB, 0)
    SB_POINTS = 2 * (40 + 2)
    CA_POINTS = 28 + 1
    for idx, (kind, h) in enumerate(heads):
        gen = make_gen(idx + 1) if idx + 1 < len(heads) else None
        if kind == "sb":
            pl = Puller(gen, SB_POINTS)
            sb_attention(h, idx % 2, pl)
        else:
            pl = Puller(gen, CA_POINTS)
            ca_attention(h, idx % 2, h % 2, pl)
        pl.flush()

    for j in range(DC):
        ug, ugn = load_unit(U_GSB + j)
        uo, uon = load_unit(U_OUTS + j)
        uc, ucn = load_unit(U_GCA + j)
        for tg in range(4):
            MM(banks[tg][:], [(ug[:, cs(c)], h1T[:, c, ts(tg)]) for c in range(DC)], [ugn, f"h1T.{tg}"], [bkn(tg)])
            ACT(sg[:, ts(tg)], banks[tg][:], AF.Sigmoid, [bkn(tg)], [f"sg.{tg}"])
        for tg in range(4):
            MM(banks[4 + tg][:], [(uo[:, cs(c)], yT[:, c, ts(tg)]) for c in range(8)], [uon, f"yT.{tg}"], [bkn(4 + tg)])
            TT(mbuf[:, ts(tg)], sg[:, ts(tg)], banks[4 + tg][:], ALU.mult, [f"sg.{tg}", bkn(4 + tg)], [f"mbuf.{tg}"])
        for tg in range(4):
            MM(banks[tg][:], [(uc[:, cs(c)], h1T[:, c, ts(tg)]) for c in range(DC)], [ucn, f"h1T.{tg}"], [bkn(tg)])
            ACT(sgc[:, ts(tg)], banks[tg][:], AF.Sigmoid, [bkn(tg)], [f"sgc.{tg}"])
        for tg in range(4):
            MM(banks[4 + tg][:], [(uo[:, cs(8 + c)], yT[:, 8 + c, ts(tg)]) for c in range(8)], [uon, f"yT.{tg}"],
               [bkn(4 + tg)])
            TT(sgc[:, ts(tg)], sgc[:, ts(tg)], banks[4 + tg][:], ALU.mult, [f"sgc.{tg}", bkn(4 + tg)], [f"sgc.{tg}"])
            TT(mgo[j % 2][:, ts(tg)], mbuf[:, ts(tg)], sgc[:, ts(tg)], ALU.add, [f"mbuf.{tg}", f"sgc.{tg}"],
               [f"mgo{j % 2}"])
        DMA("sp", mscr[j], mgo[j % 2][:], f"mst{j % 2}", [f"mgo{j % 2}"], [f"M.{j}"])
    P.barrier()

    def r12(lo, hi):
        return [f"r{i}" for i in range(lo, hi)]

    def x1n(j, t2):
        return f"x1.{j}.{t2}"

    rF = [rbufB, rbufF]
    rFn = ["rbufB", "rbufF"]
    oi = [0]

    def final_out(hb, j, tail=False):
        lst = stg_tail if tail else stg_loop
        buf, nms, ds = lst[oi[0] % len(lst)]
        oi[0] += 1
        for t2 in range(2):
            STT(buf[:, ts(t2)], x1[:, j, ts(t2)], gains[:, 48 + j:49 + j], rF[t2][:], ALU.mult, ALU.mult,
                [x1n(j, t2), rFn[t2], "gains"], nms)
        DMA("sp", outT[j][:, hb * 1024:(hb + 1) * 1024], buf[:], ds, nms, [])

    sli = 0
    sqi = [0]

    def stat_act(j, t2):
        q_ = sqi[0] % 2
        sqi[0] += 1
        ACT(sqB[q_][:, 0, :], x1[:, j, ts(t2)], AF.Square, [x1n(j, t2)], [f"sqB{q_}"])
        return q_

    def stat_pe(j, t2, q_):
        MM(banks[6 + t2][:], [(ones_bf[:], sqB[q_][:, 0, :])], [f"sqB{q_}", "ones"], [bkn(6 + t2)],
           start=(j == 0), stop=(j == DC - 1))

    class Stats:
        def __init__(self):
            self.pend = None
            self.q = None

        def begin(self):
            if self.pend is not None:
                self.q = stat_act(*self.pend)

        def mid(self):
            if self.pend is not None:
                stat_pe(*self.pend, self.q)

        def end(self, j, t2):
            self.pend = (j, t2)

        def finish(self):
            stat_pe(*self.pend, stat_act(*self.pend))
            self.pend = None
            for t2 in range(2):
                ACT(lnvB[:], banks[6 + t2][:], AF.Ln, [bkn(6 + t2)], ["lnvB"], bias=EPS, scale=1.0 / D)
                ACT(rF[t2][:], lnvB[:], AF.Exp, ["lnvB"], [rFn[t2]], scale=-0.5)

    def apply_norm(gidx):
        for t2 in range(2):
            for c in range(DC):
                STT(h2[:, c, ts(t2)], x1[:, c, ts(t2)], gains[:, gidx * 16 + c:gidx * 16 + c + 1], rF[t2][:],
                    ALU.mult, ALU.mult, [x1n(c, t2), rFn[t2], "gains"], [f"h2.{c}.{t2}"])

    def h2n(t2):
        return [f"h2.{c}.{t2}" for c in range(DC)]

    def mm_first(bank_i, wt, wname, t2):
        for c in range(DC):
            MM(banks[bank_i][:], [(wt[:, cs(c)], h2[:, c, ts(t2)])], [wname, f"h2.{c}.{t2}"], [bkn(bank_i)],
               start=(c == 0), stop=(c == DC - 1))

    def t2_first_order(n_items, lead=3):
        steps = [(i, 0) for i in range(lead)] + [(i, 1) for i in range(lead)]
        for i in range(lead, n_items):
            steps += [(i, 0), (i, 1)]
        return steps

    for hb in range(2):
        hsl = slice(hb * 1024, (hb + 1) * 1024)
        DMA("sp", mg[:], mscr_v[:, :, hsl], "mgld", [f"M.{j}" for j in range(DC)], ["actreg"])
        st = Stats()
        DMA("sp", xc[0][:], xT[0][:, hsl], "xc0", [], r12(0, 2))
        for j in range(DC):
            if j + 1 < DC:
                xn = (j + 1) % 2
                DMA("sp", xc[xn][:], xT[j + 1][:, hsl], f"xc{xn}", [], r12(2 * xn, 2 * xn + 2))
            if hb > 0:
                final_out(hb - 1, j)
            um, umn = load_unit(U_MIX + j)
            xk = j % 2
            for t2 in range(2):
                bi = (2 * j + t2) % 6
                st.begin()
                MM(banks[bi][:], [(um[:, cs(c)], mg[:, c, ts(t2)]) for c in range(DC)], [umn, "actreg"], [bkn(bi)])
                st.mid()
                TT(x1[:, j, ts(t2)], xc[xk][:, ts(t2)], banks[bi][:], ALU.add, r12(2 * xk, 2 * xk + 2) + [bkn(bi)],
                   [x1n(j, t2)])
                st.end(j, t2)
        st.finish()
        apply_norm(1)
        for hf in range(2):
            units = {}
            steps = t2_first_order(22) if hf == 0 else [(jj, t2) for jj in range(22) for t2 in range(2)]
            for jj, t2 in steps:
                j = hf * 22 + jj
                if jj not in units:
                    units[jj] = (load_unit(U_FFI + 2 * j), load_unit(U_FFI + 2 * j + 1))
                (ugj, ugn), (uuj, uun) = units[jj]
                bg = (jj % 2) * 4 + t2
                bu = (jj % 2) * 4 + 2 + t2
                if hf == 0 and (jj, t2) == (0, 0):
                    for c in range(DC):
                        MM(banks[bg][:], [(ugj[:, cs(c)], h2[:, c, ts(t2)])], [ugn, f"h2.{c}.{t2}"], [bkn(bg)],
                           start=(c == 0), stop=(c == DC - 1))
                        MM(banks[bu][:], [(uuj[:, cs(c)], h2[:, c, ts(t2)])], [uun, f"h2.{c}.{t2}"], [bkn(bu)],
                           start=(c == 0), stop=(c == DC - 1))
                else:
                    MM(banks[bg][:], [(ugj[:, cs(c)], h2[:, c, ts(t2)]) for c in range(DC)], [ugn] + h2n(t2), [bkn(bg)])
                    MM(banks[bu][:], [(uuj[:, cs(c)], h2[:, c, ts(t2)]) for c in range(DC)], [uun] + h2n(t2), [bkn(bu)])
                s_ = sli % 2
                sli += 1
                ACT(sl_t[s_][:], banks[bg][:], AF.Silu, [bkn(bg)], [f"sl{s_}"])
                wr = ["actreg"] + (["sx0", "sx1", "sx2"] if jj >= 16 else [])
                TT(act[:, jj, ts(t2)], sl_t[s_][:], banks[bu][:], ALU.mult, [f"sl{s_}", bkn(bu)], wr)
            st = Stats() if hf == 1 else None
            for j in range(DC):
                u0, u0n = load_unit(U_FFO + (hf * 16 + j) * 2, 1408)
                u1, u1n = load_unit(U_FFO + (hf * 16 + j) * 2 + 1, 1408)
                for t2 in range(2):
                    bi = (2 * j + t2) % 6
                    prs = [(u0[:, cs(c)], act[:, c, ts(t2)]) for c in range(11)] + \
                          [(u1[:, cs(c)], act[:, 11 + c, ts(t2)]) for c in range(11)]
                    if st:
                        st.begin()
                    MM(banks[bi][:], prs, [u0n, u1n, "actreg", "sx0", "sx1", "sx2"], [bkn(bi)])
                    if st:
                        st.mid()
                    TT(x1[:, j, ts(t2)], x1[:, j, ts(t2)], banks[bi][:], ALU.add, [x1n(j, t2), bkn(bi)], [x1n(j, t2)])
                    if st:
                        st.end(j, t2)
            if st:
                st.finish()
        apply_norm(2)
        DMA("sp", pTf[:], pT_v[:, :, hsl], "pT", [], r12(0, 4))
        for c in range(2):
            ACT(pTb[:, c, :], pTf[:, c, :], AF.Copy, r12(0, 4), r12(4, 6))
        st = Stats()
        units = {}
        pis = {}
        for j, t2 in t2_first_order(DC):
            if j // 8 not in pis:
                if j // 8 == 1:
                    unpin(pis[0][1])
                pis[j // 8] = load_unit(U_PI + j // 8, pin=True)
            if j not in units:
                units[j] = load_unit(U_PG + j)
            ug, ugn = units[j]
            upk, upkn = pis[j // 8]
            bg = (j % 2) * 2 + t2
            bp = 4 + t2
            st.begin()
            if (j, t2) == (0, 0):
                mm_first(bg, ug, ugn, t2)
            else:
                MM(banks[bg][:], [(ug[:, cs(c)], h2[:, c, ts(t2)]) for c in range(DC)], [ugn] + h2n(t2), [bkn(bg)])
            MM(banks[bp][:], [(upk[:, cs((j % 8) * 2 + c)], pTb[:, c, ts(t2)]) for c in range(2)],
               [upkn] + r12(4, 6), [bkn(bp)])
            st.mid()
            s_ = sli % 2
            sli += 1
            ACT(sl_t[s_][:], banks[bg][:], AF.Sigmoid, [bkn(bg)], [f"sl{s_}"])
            TT(tmp_t[s_][:], sl_t[s_][:], banks[bp][:], ALU.mult, [f"sl{s_}", bkn(bp)], [f"tmp{s_}"])
            TT(x1[:, j, ts(t2)], x1[:, j, ts(t2)], tmp_t[s_][:], ALU.add, [x1n(j, t2), f"tmp{s_}"], [x1n(j, t2)])
            st.end(j, t2)
        st.finish()
        unpin(pis[1][1])
    for j in range(DC):
        final_out(1, j, tail=True)

    P.op("sp", None, extra=[o for o in P.last_dma.values()])

    P.finalize()
    engsem = {e: nc.alloc_semaphore(name=f"sem_{e}") for e in ("pe", "act", "dve", "pool")}
    dmasem = {nm: nc.alloc_semaphore(name=f"dsem_{nm}") for nm in P.dma_cnt}
    with nc.Block() as block:
        P.emit(nc, block, engsem, dmasem)
    return nc


def _unit(wsub):
    n = wsub.shape[0] // 128
    u = np.zeros((128, 2048), np.float32)
    u[:, :n * 128] = wsub.reshape(n, 128, 128).transpose(1, 0, 2).reshape(128, n * 128)
    return u


def _build_wall(w_in, w_sb_out, w_ca_out, w_mix_out, w_ffn_in, w_ffn_out, w_ple_in, w_ple_gate):
    wall = np.zeros((NU, 128, 2048), np.float32)
    for j in range(80):
        wall[j] = _unit(w_in[:, j * 128:(j + 1) * 128])
    outs = np.concatenate([w_sb_out, w_ca_out], axis=0)
    for j in range(16):
        wall[U_OUTS + j] = _unit(outs[:, j * 128:(j + 1) * 128])
        wall[U_MIX + j] = _unit(w_mix_out[:, j * 128:(j + 1) * 128])
        wall[U_PG + j] = _unit(w_ple_gate[:, j * 128:(j + 1) * 128])
    for j in range(FC):
        wall[U_FFI + 2 * j] = _unit(w_ffn_in[:, j * 128:(j + 1) * 128])
        wall[U_FFI + 2 * j + 1] = _unit(w_ffn_in[:, DFF + j * 128:DFF + (j + 1) * 128])
    for hf in range(2):
        for j in range(16):
            for part in range(2):
                r0 = (hf * 22 + part * 11) * 128
                wall[U_FFO + (hf * 16 + j) * 2 + part] = _unit(w_ffn_out[r0:r0 + 11 * 128, j * 128:(j + 1) * 128])
    for half in range(2):
        u = np.zeros((128, 2048), np.float32)
        for jj in range(8):
            j = half * 8 + jj
            blk = w_ple_in[:, j * 128:(j + 1) * 128]
            u[:, jj * 256:(jj + 1) * 256] = blk.reshape(2, 128, 128).transpose(1, 0, 2).reshape(128, 256)
        wall[U_PI + half] = u
    return wall


def _build_consts(g_mix, g_ffn, g_ple, g_final, rel_bias):
    cst = np.zeros((128, 2496), np.float32)
    cst[:, 2368:2496] = -1.0
    cst[:, 0:128] = 1.0
    jj = np.arange(128)[:, None]
    ss = np.arange(128)[None, :]
    cst[:, 128:256] = -(jj >= ss).astype(np.float32)
    p = np.arange(128)[:, None]
    t = np.arange(512)[None, :]
    for r in range(4):
        cst[:, 256 + 512 * r:256 + 512 * (r + 1)] = (t > 128 * r + p).astype(np.float32)
    for i, g in enumerate((g_mix, g_ffn, g_ple, g_final)):
        cst[:, 2304 + 16 * i:2304 + 16 * (i + 1)] = np.asarray(g, np.float32).reshape(16, 128).T
    pj = np.arange(128)[:, None]
    jq = np.arange(640)[None, :]
    idx = np.clip(pj - jq, -128, 63) + 128
    dchunk = (jq // 64) - (pj // 64)
    valid = (dchunk >= 0) & (dchunk <= 8)
    bt = np.where(valid[None], rel_bias[:, idx], np.float32(NEGM)).astype(np.float32)
    return cst, bt


_NC_CACHE = {}


def kernel(x, p, w_in, w_sb_out, w_ca_out, w_mix_out, rel_bias, g_mix, g_ffn, g_ple, g_final,
           w_ffn_in, w_ffn_out, w_ple_in, w_ple_gate):
    f = lambda a: np.asarray(a, dtype=np.float32)
    x = f(x)
    p = f(p)
    wall = _build_wall(f(w_in)[0], f(w_sb_out)[0], f(w_ca_out)[0], f(w_mix_out)[0], f(w_ffn_in)[0],
                       f(w_ffn_out)[0], f(w_ple_in)[0], f(w_ple_gate)[0])
    cst, bt = _build_consts(f(g_mix)[0], f(g_ffn)[0], f(g_ple)[0], f(g_final), f(rel_bias)[0])
    if "nc" not in _NC_CACHE:
        _NC_CACHE["nc"] = build_program()
    nc = _NC_CACHE["nc"]
    in_maps = []
    for b in range(NCORES):
        xTb = np.ascontiguousarray(x[b].T).reshape(DC, 128, S)
        pTb = np.ascontiguousarray(p[0, b].T).reshape(2, 128, S)
        in_maps.append({"xT": xTb, "pT": pTb, "wall": wall, "cst": cst, "bt": bt})
    res = run_bass_kernel_spmd(nc, in_maps, core_ids=list(range(NCORES)))
    out = np.empty((NCORES, S, D), np.float32)
    for b in range(NCORES):
        out[b] = res.results[b]["outT"].reshape(D, S).T
    return out
```
